# Optimizing a Trainium2 kernel written in Bass

```python
import math
import jax, jax.numpy as jnp
from jax import lax
import numpy as np

D_MODEL = 1024
BATCH = 4
SEQ = 4096
DEPTH = 1

HG_HEADS = 8
HG_DK = 128
HG_DV = 128
HG_WIDTH = HG_HEADS * HG_DK
HG_CHUNK = 64
AT_HEADS = 8
AT_DH = 128
AT_WIDTH = AT_HEADS * AT_DH
MOBA_BLOCK = 256
MOBA_TOPK = 3
Q_CHUNK = 16
REL_BUCKETS = 32
REL_MAX_DIST = 128
D_FF = -(-8 * D_MODEL // (3 * 256)) * 256
N_BRANCH = 2
NORM_EPS = 1e-6
IN_SIZES = [HG_WIDTH, HG_WIDTH, HG_HEADS * HG_DV, HG_HEADS * HG_DV,
            AT_WIDTH, AT_WIDTH, AT_WIDTH, N_BRANCH * D_MODEL]
IN_COLS = sum(IN_SIZES)
IN_SPLITS = [int(s) for s in np.cumsum(IN_SIZES)[:-1]]

kernel_name = "hgrn2_moba_gated_hybrid"


def rmsnorm(x, g):
    xf = x.astype(jnp.float32)
    y = xf * lax.rsqrt(jnp.mean(xf * xf, axis=-1, keepdims=True) + NORM_EPS)
    return (y * g.astype(jnp.float32)).astype(x.dtype)


def t5_bucket(dist):
    n = jnp.maximum(dist, 0)
    max_exact = REL_BUCKETS // 2
    nf = jnp.maximum(n, 1).astype(jnp.float32)
    large = max_exact + (jnp.log(nf / max_exact) / math.log(REL_MAX_DIST / max_exact)
                         * (REL_BUCKETS - max_exact)).astype(jnp.int32)
    large = jnp.minimum(large, REL_BUCKETS - 1)
    return jnp.where(n < max_exact, n, large)


def hgrn2_mix(q, f_logit, i, g, lb, out_gain):
    B, S, _ = q.shape
    f32 = jnp.float32
    C = HG_CHUNK
    N = S // C
    z = f_logit.astype(f32)
    f = lb + (1.0 - lb) * jax.nn.sigmoid(z)
    log_f = jnp.log(f)
    k = (1.0 - lb) * jax.nn.sigmoid(-z)
    qs = q.astype(f32) * HG_DK ** -0.5

    def to_chunks(t, d):
        return t.reshape(B, N, C, HG_HEADS, d).transpose(1, 0, 3, 2, 4)

    qc = to_chunks(qs, HG_DK)
    kc = to_chunks(k, HG_DK)
    lfc = to_chunks(log_f, HG_DK)
    vc = to_chunks(i.astype(f32), HG_DV)
    causal = jnp.tril(jnp.ones((C, C), dtype=bool))

    def step(state, inp):
        q_, k_, v_, lf_ = inp
        b = jnp.cumsum(lf_, axis=2)
        o_inter = jnp.einsum('bhtk,bhkv->bhtv', q_ * jnp.exp(b), state)
        diff = b[:, :, :, None, :] - b[:, :, None, :, :]
        decay = jnp.exp(jnp.where(causal[:, :, None], diff, -jnp.inf))
        attn = jnp.einsum('bhtk,bhsk,bhtsk->bhts', q_, k_, decay)
        o = o_inter + jnp.einsum('bhts,bhsv->bhtv', attn, v_)
        b_last = b[:, :, -1:, :]
        k_dec = k_ * jnp.exp(b_last - b)
        new_state = (jnp.exp(b_last[:, :, 0, :])[..., None] * state
                     + jnp.einsum('bhsk,bhsv->bhkv', k_dec, v_))
        return new_state, o

    s0 = jnp.zeros((B, HG_HEADS, HG_DK, HG_DV), f32)
    _, o = lax.scan(step, s0, (qc, kc, vc, lfc))
    o = o.transpose(1, 0, 3, 2, 4).reshape(B, S, HG_HEADS, HG_DV)
    o = rmsnorm(o, out_gain) * jax.nn.silu(g.astype(f32).reshape(B, S, HG_HEADS, HG_DV))
    return o.reshape(B, S, HG_HEADS * HG_DV).astype(q.dtype)


def moba_attention(q, k, v, rel_table):
    B, H, S, Dh = q.shape
    f32 = jnp.float32
    BLK = MOBA_BLOCK
    NB = -(-S // BLK)
    pad = NB * BLK - S
    kp = jnp.pad(k, ((0, 0), (0, 0), (0, pad), (0, 0)))
    vp = jnp.pad(v, ((0, 0), (0, 0), (0, pad), (0, 0)))
    kb = kp.reshape(B, H, NB, BLK, Dh)
    vb = vp.reshape(B, H, NB, BLK, Dh)
    kmean = jnp.mean(kb.astype(f32), axis=3)
    topk = min(MOBA_TOPK, NB)
    scale = Dh ** -0.5
    table_h = rel_table.astype(f32).T
    b_idx = jnp.arange(B)[:, None, None, None]
    h_idx = jnp.arange(H)[None, :, None, None]
    h_idx5 = jnp.arange(H)[None, :, None, None, None]
    n_chunks = S // Q_CHUNK

    def one_chunk(c):
        t0 = c * Q_CHUNK
        qc = lax.dynamic_slice_in_dim(q, t0, Q_CHUNK, axis=2)
        tpos = t0 + jnp.arange(Q_CHUNK)
        cur = t0 // BLK
        gate = jnp.einsum('bhqd,bhnd->bhqn', qc.astype(f32), kmean)
        gate = jnp.where(jnp.arange(NB) < cur, gate, -jnp.inf)
        _, sel = lax.top_k(gate, topk)
        sel_valid = sel < cur
        ks = kb[b_idx, h_idx, sel]
        vs = vb[b_idx, h_idx, sel]
        lp = jnp.einsum('bhqd,bhqjsd->bhqjs', qc, ks).astype(f32) * scale
        kpos = sel[..., None] * BLK + jnp.arange(BLK)
        bias_p = table_h[h_idx5, t5_bucket(tpos[:, None, None] - kpos)]
        lp = jnp.where(sel_valid[..., None], lp + bias_p, -jnp.inf)
        ko = lax.dynamic_slice_in_dim(kp, cur * BLK, BLK, axis=2)
        vo = lax.dynamic_slice_in_dim(vp, cur * BLK, BLK, axis=2)
        lo = jnp.einsum('bhqd,bhsd->bhqs', qc, ko).astype(f32) * scale
        dist_o = tpos[:, None] - (cur * BLK + jnp.arange(BLK))[None, :]
        lo = jnp.where(dist_o >= 0, lo + table_h[:, t5_bucket(dist_o)], -jnp.inf)
        logits = jnp.concatenate([lo, lp.reshape(B, H, Q_CHUNK, topk * BLK)], axis=-1)
        p = jax.nn.softmax(logits, axis=-1)
        p_own = p[..., :BLK].astype(v.dtype)
        p_past = p[..., BLK:].reshape(B, H, Q_CHUNK, topk, BLK).astype(v.dtype)
        return (jnp.einsum('bhqs,bhsd->bhqd', p_own, vo)
                + jnp.einsum('bhqjs,bhqjsd->bhqd', p_past, vs))

    out = lax.map(one_chunk, jnp.arange(n_chunks))
    return out.transpose(1, 2, 0, 3, 4).reshape(B, H, S, Dh)


def setup_inputs(seed: int = 0) -> dict:
    key = jax.random.key(seed)
    ks = jax.random.split(key, 16)
    f32 = jnp.float32
    nrm = lambda k, shape, s: jax.random.normal(k, shape, f32) * s
    return {
        "x": jax.random.normal(ks[0], (BATCH, SEQ, D_MODEL), f32),
        "attn_norm_g": 1.0 + nrm(ks[1], (DEPTH, D_MODEL), 0.02),
        "w_in": nrm(ks[2], (DEPTH, D_MODEL, IN_COLS), D_MODEL ** -0.5),
        "hg_lb_gamma": nrm(ks[3], (DEPTH + 1, HG_WIDTH), 0.5),
        "hg_out_norm_g": 1.0 + nrm(ks[4], (DEPTH, HG_DV), 0.02),
        "q_norm_g": 1.0 + nrm(ks[5], (DEPTH, AT_DH), 0.02),
        "k_norm_g": 1.0 + nrm(ks[6], (DEPTH, AT_DH), 0.02),
        "rel_bias_table": nrm(ks[7], (REL_BUCKETS, AT_HEADS), 0.2),
        "w_branch_hg": nrm(ks[8], (DEPTH, HG_HEADS * HG_DV, D_MODEL), (HG_HEADS * HG_DV) ** -0.5),
        "w_branch_attn": nrm(ks[9], (DEPTH, AT_WIDTH, D_MODEL), AT_WIDTH ** -0.5),
        "w_out": nrm(ks[10], (DEPTH, D_MODEL, D_MODEL), D_MODEL ** -0.5),
        "ffn_norm_g": 1.0 + nrm(ks[11], (DEPTH, D_MODEL), 0.02),
        "w_ffn_gate": nrm(ks[12], (DEPTH, D_MODEL, D_FF), D_MODEL ** -0.5),
        "w_ffn_up": nrm(ks[13], (DEPTH, D_MODEL, D_FF), D_MODEL ** -0.5),
        "w_ffn_down": nrm(ks[14], (DEPTH, D_FF, D_MODEL), D_FF ** -0.5),
    }


def reference(x, attn_norm_g, w_in, hg_lb_gamma, hg_out_norm_g, q_norm_g, k_norm_g,
              rel_bias_table, w_branch_hg, w_branch_attn, w_out, ffn_norm_g,
              w_ffn_gate, w_ffn_up, w_ffn_down):
    B, S, D = x.shape
    lb_all = jnp.cumsum(jax.nn.softmax(hg_lb_gamma.astype(jnp.float32), axis=0), axis=0)[:DEPTH]
    for l in range(DEPTH):
        h = rmsnorm(x, attn_norm_g[l])
        proj = h @ w_in[l]
        hq, hf, hi, hg, aq, ak, av, gl = jnp.split(proj, IN_SPLITS, axis=-1)
        y_hg = hgrn2_mix(hq, hf, hi, hg, lb_all[l], hg_out_norm_g[l]) @ w_branch_hg[l]
        qh = rmsnorm(aq.reshape(B, S, AT_HEADS, AT_DH), q_norm_g[l]).transpose(0, 2, 1, 3)
        kh = rmsnorm(ak.reshape(B, S, AT_HEADS, AT_DH), k_norm_g[l]).transpose(0, 2, 1, 3)
        vh = av.reshape(B, S, AT_HEADS, AT_DH).transpose(0, 2, 1, 3)
        att = moba_attention(qh, kh, vh, rel_bias_table)
        y_at = att.transpose(0, 2, 1, 3).reshape(B, S, AT_WIDTH) @ w_branch_attn[l]
        gates = jax.nn.sigmoid(gl.reshape(B, S, N_BRANCH, D))
        merged = gates[:, :, 0, :] * y_hg + gates[:, :, 1, :] * y_at
        x = x + merged @ w_out[l]
        h2 = rmsnorm(x, ffn_norm_g[l])
        x = x + (jax.nn.silu(h2 @ w_ffn_gate[l]) * (h2 @ w_ffn_up[l])) @ w_ffn_down[l]
    return x
```

```python
import contextlib
import math
import numpy as np
import concourse.bass as bass
import concourse.mybir as mybir
from concourse.bass_utils import run_bass_kernel_spmd

F32 = mybir.dt.float32
BF16 = mybir.dt.bfloat16
AF = mybir.ActivationFunctionType
ALU = mybir.AluOpType
AX = mybir.AxisListType

D = 1024
NPRE = 2048
NOWN = 2048
NTOK = NPRE + NOWN
H = 8
DH = 128
DFF = 2816
NFT = DFF // 128
EPS = 1e-6
NEG = -30000.0
C_HQ, C_HF, C_HI, C_HG, C_AQ, C_AK, C_AV, C_GL = 0, 1024, 2048, 3072, 4096, 5120, 6144, 7168
INC = 9216
ARENA_WORDS = 20096


class Sched:
    CAP = 30000
    NDMA = 8

    def __init__(self):
        self.ops = []

    def add(self, eng, fn, r=(), w=(), dma=False):
        self.ops.append(dict(eng=eng, fn=fn, r=tuple(r), w=tuple(w), dma=dma))

    def barrier(self):
        self.ops.append(None)

    def emit(self, nc, stack):
        ops = self.ops
        engs = ["pe", "act", "dve", "pool", "sp"]
        last_w = {}
        readers = {}
        cur_last = {}
        cur_dmas = []
        pending = {}
        for i, op in enumerate(ops):
            if op is None:
                B = set(cur_last.values()) | set(cur_dmas)
                for e in engs:
                    pending[e] = pending.get(e, set()) | B
                cur_dmas = []
                continue
            deps = set()
            for k in op["r"]:
                if k in last_w:
                    deps.add(last_w[k])
            for k in op["w"]:
                if k in last_w:
                    deps.add(last_w[k])
                for j in readers.get(k, ()):
                    deps.add(j)
            e = op["eng"]
            if pending.get(e):
                deps |= pending.pop(e)
            deps.discard(i)
            op["deps"] = deps
            for k in op["r"]:
                readers.setdefault(k, []).append(i)
            for k in op["w"]:
                last_w[k] = i
                readers[k] = []
            if op["dma"]:
                cur_dmas.append(i)
            else:
                cur_last[e] = i
        cnt = {e: 0 for e in engs}
        dcnt = {e: 0 for e in engs}
        sems = {}

        def get_sem(name):
            if name not in sems:
                sems[name] = stack.enter_context(nc.semaphore(name))
            return sems[name]

        for op in ops:
            if op is None:
                continue
            e = op["eng"]
            if op["dma"]:
                j = dcnt[e]
                dcnt[e] += 1
                slot = j % self.NDMA
                op["sig"] = (get_sem(f"d_{e}_{slot}"), 16 * (j // self.NDMA + 1))
                op["prev"] = (get_sem(f"d_{e}_{slot}"), 16 * (j // self.NDMA))
            else:
                k = cnt[e]
                cnt[e] += 1
                op["sig"] = (get_sem(f"c_{e}_{k // self.CAP}"), (k % self.CAP) + 1)
        per_eng = {e: [] for e in engs}
        for i, op in enumerate(ops):
            if op is not None:
                per_eng[op["eng"]].append(i)
        self.stats = {e: len(per_eng[e]) for e in engs}

        block = stack.enter_context(nc.Block())

        def run_engine(eng_name):
            def body(eng):
                waited = {}
                for i in per_eng[eng_name]:
                    op = ops[i]
                    waits = []
                    for d in sorted(op["deps"]):
                        dop = ops[d]
                        if dop["eng"] == "pe" and eng_name == "pe" and not dop["dma"]:
                            continue
                        waits.append(dop["sig"])
                    if op["dma"] and op["prev"][1] > 0:
                        waits.append(op["prev"])
                    for sem, val in waits:
                        key = id(sem)
                        if waited.get(key, 0) >= val:
                            continue
                        waited[key] = val
                        eng.wait_ge(sem, val)
                    ins = op["fn"](eng)
                    sem, val = op["sig"]
                    ins.then_inc(sem, 16 if op["dma"] else 1)
                for slot in range(self.NDMA):
                    name = f"d_{eng_name}_{slot}"
                    if name in sems:
                        n = (dcnt[eng_name] - slot + self.NDMA - 1) // self.NDMA
                        if n > 0:
                            eng.wait_ge(sems[name], 16 * n)
            return body

        block.sync(run_engine("sp"))
        block.tensor(run_engine("pe"))
        block.scalar(run_engine("act"))
        block.vector(run_engine("dve"))
        block.gpsimd(run_engine("pool"))


class Arena:
    def __init__(self, t, words):
        self.t = t
        self.words = words
        self.off = 0

    def reset(self):
        self.off = 0

    def alloc(self, shape, dt):
        n = 1
        for d_ in shape[1:]:
            n *= d_
        w = (n + 1) // 2 if dt == BF16 else n
        w = (w + 7) // 8 * 8
        assert self.off + w <= self.words, ("arena overflow", self.off, w, self.words)
        ap = self.t[:, self.off:self.off + w]
        self.off += w
        if dt == BF16:
            ap = ap.bitcast(BF16)
        ap = ap[:, 0:n]
        if len(shape) == 3:
            ap = ap.rearrange("p (a b) -> p a b", a=shape[1])
        elif len(shape) == 4:
            ap = ap.rearrange("p (a b c) -> p a b c", a=shape[1], b=shape[2])
        if shape[0] != 128:
            ap = ap[0:shape[0]]
        return ap


def build_program(stop_after=None, taps=(), heads=None):
    nc = bass.Bass("TRN2", target_bir_lowering=False)
    s = Sched()
    stack = contextlib.ExitStack()
    heads = list(range(H)) if heads is None else list(heads)

    def din(name, shape, dt=F32):
        return nc.dram_tensor(name, list(shape), dt, kind="ExternalInput").ap()

    def dout(name, shape, dt=F32):
        return nc.dram_tensor(name, list(shape), dt, kind="ExternalOutput").ap()

    x_pre = din("x_pre", [NPRE, D])
    x_own = din("x_own", [NOWN, D])
    attn_g = din("attn_norm_g", [1, D])
    w_in = din("w_in", [D, INC])
    lbg = din("hg_lb_gamma", [2, D])
    hgg = din("hg_out_norm_g", [1, DH])
    qg = din("q_norm_g", [1, DH])
    kg = din("k_norm_g", [1, DH])
    relb = din("rel_bias_table", [32, H])
    w_bh = din("w_branch_hg", [D, D])
    w_ba = din("w_branch_attn", [D, D])
    w_out = din("w_out", [D, D])
    ffn_g = din("ffn_norm_g", [1, D])
    w_fg = din("w_ffn_gate", [D, DFF])
    w_fu = din("w_ffn_up", [D, DFF])
    w_fd = din("w_ffn_down", [DFF, D])
    vm_in = din("vmask", [128, 16, 16])
    vm2_in = din("vmask2", [128, 16, 16])
    oneh_in = din("onehot", [16, 16 * 128])
    btab_in = din("btab", [H, 128, 2, 128])
    out = dout("out", [NOWN, D])
    tap_aps = {}
    for name, shape, dt in taps:
        tap_aps[name] = dout("tap_" + name, shape, dt)

    def sb(name, shape, dt):
        return stack.enter_context(nc.sbuf_tensor(name, list(shape), dt))

    PS = [stack.enter_context(nc.psum_tensor(f"ps{i}", [128, 512], F32)) for i in range(8)]

    def dma(o, i):
        return lambda e: e.dma_start(out=o, in_=i)

    def mmgroup(out_ap, pairs):
        def fn(pe):
            last = None
            n = len(pairs)
            for i, (l, r_) in enumerate(pairs):
                last = pe.matmul(out_ap, l, r_, start=(i == 0), stop=(i == n - 1))
            return last
        return fn

    def finish():
        s.emit(nc, stack)
        stack.close()
        build_program.stats = s.stats
        return nc

    ident = sb("ident", [128, 128], BF16)
    ones_bf = sb("ones_bf", [128, 128], BF16)
    identf = sb("identf", [128, 128], F32)
    neghalf = sb("neghalf", [128, 1], F32)
    hT_own = sb("hT_own", [128, 8, NOWN], BF16)
    hT_pre = sb("hT_pre", [128, 8, NPRE], BF16)
    ohgT = sb("ohgT", [128, H, NOWN], BF16)
    attT = sb("attT", [128, H, NOWN], BF16)
    arena_t = sb("arena", [128, ARENA_WORDS], F32)
    ar = Arena(arena_t, ARENA_WORDS)

    s.add("pool", lambda e: e.memset(identf[:], 0.0), w=["identf"])
    s.add("pool", lambda e: e.affine_select(out=identf[:], in_=identf[:], pattern=[[-1, 128]],
                                            compare_op=ALU.not_equal, fill=1.0, base=0,
                                            channel_multiplier=1), r=["identf"], w=["identf"])
    s.add("pool", lambda e: e.tensor_copy(out=ident[:], in_=identf[:]), r=["identf"], w=["ident"])
    s.add("pool", lambda e: e.memset(ones_bf[:], 1.0), w=["ones_bf"])
    s.add("pool", lambda e: e.memset(neghalf[:], -0.5), w=["neghalf"])

    def hT(kt, t0, n):
        if t0 < NPRE:
            return hT_pre[:, kt, t0:t0 + n]
        return hT_own[:, kt, t0 - NPRE:t0 - NPRE + n]

    def hT3(t0, n):
        if t0 < NPRE:
            return hT_pre[:, :, t0:t0 + n]
        return hT_own[:, :, t0 - NPRE:t0 - NPRE + n]

    def hkeys(t0, n):
        return [("hT", t) for t in range(t0 // 128, (t0 + n) // 128)]

    def norm_transpose(xtile, xkey, gBt, ssq_col, rstd_col, xn_t, xnkey, junk_t, bank, dst3, dkey, skey, defer_tr=False,
                       copy_eng="act"):
        s.add("act", lambda e: e.activation(out=junk_t, in_=xtile, func=AF.Square, accum_out=ssq_col),
              r=list(xkey), w=["junk", ("ssq", skey)])
        s.add("act", lambda e: e.activation(out=rstd_col, in_=ssq_col, func=AF.Ln, scale=1.0 / D, bias=EPS),
              r=[("ssq", skey)], w=[("rstd1", skey)])
        s.add("act", lambda e: e.activation(out=rstd_col, in_=rstd_col, func=AF.Exp, scale=-0.5),
              r=[("rstd1", skey)], w=[("rstd1", skey)])
        s.add("dve", lambda e: e.scalar_tensor_tensor(out=xn_t, in0=xtile, scalar=rstd_col, in1=gBt,
                                                      op0=ALU.mult, op1=ALU.mult),
              r=list(xkey) + [("rstd1", skey), "gB"], w=[xnkey])

        def tr(pe):
            pv = PS[bank][:].bitcast(BF16)
            last = None
            for kt in range(8):
                last = pe.transpose(out=pv[:, kt * 128:(kt + 1) * 128], in_=xn_t[:, kt * 128:(kt + 1) * 128],
                                    identity=ident[:])
            return last
        if not defer_tr:
            s.add("pe", tr, r=[xnkey, "ident"], w=[("ps", bank)])

        def fin():
            if defer_tr:
                s.add("pe", tr, r=[xnkey, "ident"], w=[("ps", bank)])
            pv3 = PS[bank][:].bitcast(BF16)[:, 0:1024].rearrange("p (k t) -> p k t", k=8)
            if copy_eng == "dve":
                s.add("dve", lambda e: e.tensor_copy(out=dst3, in_=pv3), r=[("ps", bank)], w=[dkey])
            else:
                s.add("act", lambda e: e.activation(out=dst3, in_=pv3, func=AF.Copy), r=[("ps", bank)], w=[dkey])
        return fin

    ar.reset()
    gB = ar.alloc([128, D], F32)
    xt = [ar.alloc([128, D], F32) for _ in range(4)]
    xn = [ar.alloc([128, D], BF16) for _ in range(4)]
    junk = ar.alloc([128, D], BF16)
    ssq = ar.alloc([128, 32], F32)
    rstd1 = ar.alloc([128, 32], F32)
    s.add("sp", dma(gB, attn_g.partition_broadcast(128)), w=["gB"], dma=True)
    fins = []
    for ti in range(32):
        src = x_pre if ti < 16 else x_own
        r0 = (ti % 16) * 128
        b = ti % 4
        s.add("sp", dma(xt[b], src[r0:r0 + 128, :]), w=[("xt", b)], dma=True)
        fins.append(norm_transpose(xt[b], [("xt", b)], gB, ssq[:, ti:ti + 1], rstd1[:, ti:ti + 1], xn[b], ("xn", b),
                                   junk, ti % 4, hT3(ti * 128, 128), ("hT", ti), ti,
                                   copy_eng=("dve" if ti % 2 == 1 else "act")))
        if len(fins) > 3:
            fins.pop(0)()
    while fins:
        fins.pop(0)()

    if "hT" in tap_aps:
        s.add("sp", dma(tap_aps["hT"][:, :, 0:NPRE], hT_pre[:]), r=[("hT", t) for t in range(16)], dma=True)
        s.add("sp", dma(tap_aps["hT"][:, :, NPRE:NTOK], hT_own[:]), r=[("hT", t) for t in range(16, 32)], dma=True)
    if stop_after == "p1":
        return finish()

    w_in_v = w_in.rearrange("(kt p) c -> p kt c", p=128)

    s.barrier()
    ar.reset()
    gq_s = ar.alloc([128, 1], F32)
    gk_s = ar.alloc([128, 1], F32)
    m0 = ar.alloc([128, 4, 128], BF16)
    Wq3 = [ar.alloc([128, 8, 128], BF16) for _ in range(2)]
    Wk3 = [ar.alloc([128, 8, 128], BF16) for _ in range(2)]
    Wv3 = [ar.alloc([128, 8, 128], BF16) for _ in range(2)]
    Bt = [ar.alloc([128, 2, 128], BF16) for _ in range(2)]
    KT = ar.alloc([128, NTOK], BF16)
    QT = ar.alloc([128, NOWN], BF16)
    Vaug = ar.alloc([128, 32, 130], BF16)
    sq3 = [ar.alloc([128, 512], BF16) for _ in range(2)]
    rs3 = [ar.alloc([128, 512], F32) for _ in range(2)]
    kmf = ar.alloc([128, 16], F32)
    kmT = ar.alloc([128, 16], BF16)
    vm = ar.alloc([128, 16, 16], F32)
    vm2 = ar.alloc([128, 16, 16], F32)
    b31 = ar.alloc([128, H], F32)
    oneh = ar.alloc([128, 16 * 128], BF16)
    maskT0 = ar.alloc([128, NOWN], BF16)
    maskTb = ar.alloc([128, NOWN], BF16)
    gm = ar.alloc([128, 4, 16], F32)
    top8 = ar.alloc([128, 4, 8], F32)
    selt = ar.alloc([128, 4, 16], F32)
    PT = [ar.alloc([128, 512], BF16) for _ in range(3)]
    rden = ar.alloc([128, 4], F32)
    att_tm = [ar.alloc([128, 128], BF16) for _ in range(2)]

    s.add("sp", dma(gq_s, qg.rearrange("o d -> d o")), w=["gq_s"], dma=True)
    s.add("sp", dma(gk_s, kg.rearrange("o d -> d o")), w=["gk_s"], dma=True)
    s.add("pool", lambda e: e.tensor_scalar(out=gq_s, in0=gq_s, scalar1=float(DH ** -0.5), scalar2=None,
                                            op0=ALU.mult), r=["gq_s"], w=["gq_s"])
    s.add("sp", dma(vm, vm_in), w=["vm"], dma=True)
    s.add("sp", dma(vm2, vm2_in), w=["vm2"], dma=True)
    s.add("sp", dma(b31, relb[31:32, :].partition_broadcast(128)), w=["b31"], dma=True)
    s.add("pool", lambda e: e.memset(oneh, 0.0), w=["oneh"])
    s.add("pool", dma(oneh[0:16, :], oneh_in), r=["oneh"], w=["oneh"], dma=True)
    s.add("pool", lambda e: e.memset(m0, 0.0), w=["m0"])
    s.add("pool", lambda e: e.memset(maskT0, 0.0), w=["maskT0"])
    s.add("pool", lambda e: e.memset(maskTb, 0.0), w=["maskTb"])
    s.add("pool", lambda e: e.memset(Vaug[:, :, 128:129], 1.0), w=["Vaug1"])

    pt_ctr = 0
    for hi_, h in enumerate(heads):
        p = hi_ % 2
        for (W, c0_, nm) in ((Wk3, C_AK, "Wk3"), (Wq3, C_AQ, "Wq3"), (Wv3, C_AV, "Wv3")):
            s.add("pool", dma(W[p], w_in_v[:, :, c0_ + h * 128:c0_ + (h + 1) * 128]), w=[(nm, p)], dma=True)
        s.add("pool", dma(Bt[p], btab_in[h]), w=[("Bt", p)], dma=True)

        jobs = [(Wk3, "Wk3", G * 512, KT[:, G * 512:(G + 1) * 512], ("KT", G), gk_s) for G in range(8)] + \
               [(Wq3, "Wq3", NPRE + g * 512, QT[:, g * 512:(g + 1) * 512], ("QT", g), gq_s) for g in range(4)]

        def proj_mm(ji, p=p):
            W, nm, t0, dst, dkey, gcol = jobs[ji]
            ba = (2 * ji) % 6
            s.add("pe", mmgroup(PS[ba][:], [(W[p][:, kt, :], hT(kt, t0, 512)) for kt in range(8)]),
                  r=[(nm, p)] + hkeys(t0, 512), w=[("ps", ba)])

        def proj_norm(ji):
            W, nm, t0, dst, dkey, gcol = jobs[ji]
            ba, bb = (2 * ji) % 6, (2 * ji + 1) % 6
            pa, pbk = PS[ba], PS[bb]
            q = ji % 2
            s.add("act", lambda e: e.activation(out=sq3[q], in_=pa[:], func=AF.Square),
                  r=[("ps", ba)], w=[("sq3", q)])
            s.add("pe", lambda pe: pe.matmul(pbk[:], ones_bf[:], sq3[q], start=True, stop=True),
                  r=[("sq3", q), "ones_bf"], w=[("ps", bb)])
            s.add("act", lambda e: e.activation(out=rs3[q], in_=pbk[:], func=AF.Ln, scale=1.0 / DH, bias=EPS),
                  r=[("ps", bb)], w=[("rs3", q)])
            s.add("act", lambda e: e.activation(out=rs3[q], in_=rs3[q], func=AF.Exp, scale=-0.5),
                  r=[("rs3", q)], w=[("rs3", q)])
            s.add("dve", lambda e: e.scalar_tensor_tensor(out=dst, in0=pa[:], scalar=gcol, in1=rs3[q],
                                                          op0=ALU.mult, op1=ALU.mult),
                  r=[("ps", ba), ("rs3", q), "gk_s", "gq_s"], w=[dkey])
            if ji < 8:
                s.add("dve", lambda e: e.tensor_reduce(out=kmf[:, 2 * ji:2 * ji + 2],
                                                       in_=dst.rearrange("p (n s) -> p n s", s=256),
                                                       axis=AX.X, op=ALU.add),
                      r=[dkey], w=[("kmf", ji)])

        proj_mm(0)
        for ji in range(len(jobs)):
            if ji + 1 < len(jobs):
                proj_mm(ji + 1)
            proj_norm(ji)
        if stop_after == "p3a":
            break
        s.add("dve", lambda e: e.tensor_scalar(out=kmT, in0=kmf, scalar1=1.0 / 256, scalar2=None,
                                               op0=ALU.mult), r=[("kmf", G) for G in range(8)], w=["kmT"])
        def vproj(G, p=p):
            bk = (2 * G) % 6

            def vmm(pe, G=G, bk=bk, p=p):
                last = None
                for i in range(4):
                    t0 = (4 * G + i) * 128
                    for kt in range(8):
                        last = pe.matmul(PS[bk][:, i * 128:(i + 1) * 128], hT(kt, t0, 128), Wv3[p][:, kt, :],
                                         start=(kt == 0), stop=(kt == 7))
                return last
            s.add("pe", vmm, r=[("Wv3", p)] + hkeys(G * 512, 512), w=[("ps", bk)])
            s.add("act", lambda e, G=G, bk=bk: e.activation(out=Vaug[:, 4 * G:4 * G + 4, 0:128],
                                                            in_=PS[bk][:].rearrange("p (i d) -> p i d", i=4),
                                                            func=AF.Copy),
                  r=[("ps", bk)], w=[("V", G)])

        for g in range(4):
            def gate_mm(pe, g=g):
                last = None
                for i in range(4):
                    last = pe.matmul(PS[6][:, i * 16:(i + 1) * 16], QT[:, g * 512 + i * 128:g * 512 + (i + 1) * 128],
                                     kmT, start=True, stop=True)
                return last
            DBG = 99
            if DBG < 1:
                break
            s.add("pe", gate_mm, r=[("QT", g), "kmT"], w=["ps6a"])
            if DBG < 2:
                break
            s.add("dve", lambda e, g=g: e.tensor_tensor(out=gm, in0=PS[6][:, 0:64].rearrange("p (i n) -> p i n", i=4),
                                                        in1=vm[:, 4 * g:4 * g + 4, :], op=ALU.add),
                  r=["ps6a", "vm"], w=["gm"])
            if DBG < 3:
                break
            for i in range(4):
                s.add("dve", lambda e, i=i: e.max(out=top8[:, i, :], in_=gm[:, i, :]), r=["gm"], w=[("top8", i)])
            if DBG < 4:
                break
            for i in range(4):
                s.add("dve", lambda e, i=i: e.tensor_scalar(out=selt[:, i, :], in0=gm[:, i, :],
                                                            scalar1=top8[:, i, 2:3], scalar2=30000.0,
                                                            op0=ALU.is_ge, op1=ALU.mult),
                      r=["gm", ("top8", i)], w=[("selt", i)])
            s.add("dve", lambda e, g=g: e.tensor_tensor(out=m0[:, :, 0:16], in0=selt, in1=vm2[:, 4 * g:4 * g + 4, :],
                                                        op=ALU.add),
                  r=[("selt", i) for i in range(4)] + ["vm2"], w=["m0"])

            if DBG < 5:
                break

            def mtr(pe):
                last = None
                for i in range(4):
                    last = pe.matmul(PS[5][:, i * 128:(i + 1) * 128], m0[:, i, :], ident[:], start=True, stop=True)
                return last
            vproj(2 * g)
            vproj(2 * g + 1)
            s.add("pe", mtr, r=["m0", "ident"], w=[("ps", 5)])
            if DBG < 6:
                break
            s.add("act", lambda e, g=g: e.activation(out=maskT0[:, g * 512:(g + 1) * 512],
                                                     in_=PS[5][:, :],
                                                     func=AF.Copy), r=[("ps", 5), "maskT0"], w=[("maskT0", g)])
            if DBG < 7:
                break
            s.add("act", lambda e, g=g, h=h: e.activation(out=maskTb[:, g * 512:(g + 1) * 512],
                                                          in_=PS[5][:, :], func=AF.Identity, bias=b31[:, h:h + 1]),
                  r=[("ps", 5), "b31", "maskTb"], w=[("maskTb", g)])
            if DBG < 8:
                break

        if stop_after == "p3b":
            break
        for g in range(4):
            c0 = 8 + 2 * g
            nkt = 2 * c0 + 4
            qc0 = g * 512

            def classify(n, r_, i, c0=c0):
                qb_, qi = c0 + i // 2, i % 2
                if n > qb_:
                    return "SKIP"
                if n == qb_:
                    if r_ == qi:
                        return "DIAG"
                    return "OFF" if r_ < qi else "SKIP"
                if n == qb_ - 1 and r_ == 1 and qi == 0:
                    return "NEAR"
                return "FAR"

            def qk_op(j, g=g, qc0=qc0, p=p):
                n, r_ = j // 2, j % 2
                cls = [classify(n, r_, i) for i in range(4)]
                lo = min(i for i in range(4) if cls[i] != "SKIP")
                assert all(c != "SKIP" for c in cls[lo:])
                bank = j % 2
                st = PS[bank]
                terms = [(lo * 128, 512, KT[:, j * 128:(j + 1) * 128], QT[:, qc0 + lo * 128:qc0 + 512])]
                i = lo
                while i < 4:
                    if cls[i] == "FAR":
                        k = i
                        while k < 4 and cls[k] == "FAR":
                            k += 1
                        terms.append((i * 128, k * 128, oneh[:, n * 128:(n + 1) * 128],
                                      maskTb[:, qc0 + i * 128:qc0 + k * 128]))
                        i = k
                        continue
                    if cls[i] == "NEAR":
                        terms.append((i * 128, (i + 1) * 128, oneh[:, n * 128:(n + 1) * 128],
                                      maskT0[:, qc0 + i * 128:qc0 + (i + 1) * 128]))
                        terms.append((i * 128, (i + 1) * 128, ident[:], Bt[p][:, 1, :]))
                    elif cls[i] == "DIAG":
                        terms.append((i * 128, (i + 1) * 128, ident[:], Bt[p][:, 0, :]))
                    elif cls[i] == "OFF":
                        terms.append((i * 128, (i + 1) * 128, ident[:], Bt[p][:, 1, :]))
                    i += 1

                def fn(pe):
                    last = None
                    for ti_, (a, b_, l, r2) in enumerate(terms):
                        last = pe.matmul(st[:, a:b_], l, r2, start=(ti_ == 0), stop=(ti_ == len(terms) - 1))
                    return last
                s.add("pe", fn, r=[("KT", j // 4), ("QT", g), ("maskT0", g), ("maskTb", g), ("Bt", p), "oneh", "ident"],
                      w=[("ps", bank)])
                return lo, bank

            def exp_op(j, lo, bank, pt):
                s.add("act", lambda e: e.activation(out=PT[pt][:, lo * 128:512], in_=PS[bank][:, lo * 128:512],
                                                    func=AF.Exp), r=[("ps", bank)], w=[("PT", pt)])

            def pv_op(j, lo, pt, c0=c0):
                last_js = [2 * (c0 + i // 2) + (i % 2) for i in range(4)]

                def fn(pe):
                    last = None
                    for i in range(lo, 4):
                        last = pe.matmul(PS[2 + i][:, 0:129], PT[pt][:, i * 128:(i + 1) * 128], Vaug[:, j, 0:129],
                                         start=(j == 0), stop=(j == last_js[i]))
                    return last
                s.add("pe", fn, r=[("PT", pt), ("V", j // 4), "Vaug1"], w=[("ps", 2 + i) for i in range(lo, 4)])

            DA = 99
            prev = None
            for j in range(nkt):
                lo, bank = qk_op(j)
                pt = pt_ctr % 3
                pt_ctr += 1
                exp_op(j, lo, bank, pt)
                if prev is not None and DA >= 2:
                    pv_op(*prev)
                prev = (j, lo, pt)
            if DA < 2:
                break
            pv_op(*prev)
            if DA < 3:
                break
            for i in range(4):
                a = i % 2
                s.add("dve", lambda e, i=i: e.reciprocal(out=rden[:, i:i + 1], in_=PS[2 + i][:, 128:129]),
                      r=[("ps", 2 + i)], w=[("rden", i)])
                if DA < 4:
                    continue
                s.add("act", lambda e, i=i, a=a: e.activation(out=att_tm[a], in_=PS[2 + i][:, 0:128],
                                                              func=AF.Copy, scale=rden[:, i:i + 1]),
                      r=[("ps", 2 + i), ("rden", i)], w=[("att_tm", a)])
                if DA < 5:
                    continue
                s.add("pe", lambda pe, i=i, a=a: pe.transpose(out=PS[6][:].bitcast(BF16)[:, 512 + i * 128:512 + (i + 1) * 128],
                                                              in_=att_tm[a], identity=ident[:]),
                      r=[("att_tm", a), "ident"], w=[("ps6b", i)])
            if DA < 6:
                break
            s.add("act", lambda e, g=g, h=h: e.activation(out=attT[:, h, g * 512:(g + 1) * 512],
                                                          in_=PS[6][:].bitcast(BF16)[:, 512:1024], func=AF.Copy),
                  r=[("ps6b", i) for i in range(4)], w=[("attT", h, g)])
            if DA < 7:
                break

    for nm_, t_ in (("KT", KT), ("QT", QT), ("Vaug", Vaug), ("maskT0", maskT0), ("maskTb", maskTb), ("attT", attT[:])):
        if nm_ in tap_aps:
            keys = {"KT": [("KT", G) for G in range(8)], "QT": [("QT", g) for g in range(4)],
                    "Vaug": [("V", G) for G in range(8)] + ["Vaug1"],
                    "maskT0": [("maskT0", g) for g in range(4)], "maskTb": [("maskTb", g) for g in range(4)],
                    "attT": [("attT", hh, g) for hh in heads for g in range(4)]}[nm_]
            if nm_ == "attT":
                for hh in heads:
                    for g_ in range(4):
                        s.add("sp", dma(tap_aps[nm_][:, hh, g_ * 512:(g_ + 1) * 512],
                                        attT[:, hh, g_ * 512:(g_ + 1) * 512]), r=[("attT", hh, g_)], dma=True)
            else:
                s.add("sp", dma(tap_aps[nm_], t_), r=keys, dma=True)
    if stop_after in ("p3", "p3a", "p3b"):
        return finish()

    s.barrier()
    ar.reset()
    lbraw = ar.alloc([128, 2, H], F32)
    lb_s = ar.alloc([128, H], F32)
    lnoml = ar.alloc([128, H], F32)
    tmp8 = ar.alloc([128, H], F32)
    hgain = ar.alloc([128, 1], F32)
    resetm = ar.alloc([128, 512], F32)
    tri3 = ar.alloc([128, 4, 128], BF16)
    lbrawT = ar.alloc([128, 128], F32)
    Wf2 = [ar.alloc([128, 8, 128], BF16) for _ in range(2)]
    Wi2 = [ar.alloc([128, 8, 128], BF16) for _ in range(2)]
    Wq2 = [ar.alloc([128, 8, 128], BF16) for _ in range(2)]
    Wg2 = [ar.alloc([128, 8, 128], BF16) for _ in range(2)]
    S_f = [ar.alloc([128, 128], F32) for _ in range(2)]
    S_bf = [ar.alloc([128, 128], BF16) for _ in range(2)]

    def two(shape, dt):
        return [ar.alloc(shape, dt) for _ in range(2)]
    eu_t = two([128, 512], F32)
    L1_t = two([128, 512], F32)
    L2_t = two([128, 512], F32)
    bc_t = two([128, 512], F32)
    rs_t = two([128, 512], F32)
    sg_t = two([128, 512], F32)
    m1_t = two([128, 512], F32)
    Eq_t = two([128, 512], BF16)
    Eb_t = two([128, 512], BF16)
    Eqo_t = two([128, 512], BF16)
    tmpA1 = ar.alloc([128, 512], BF16)
    tmpA = [tmpA1, tmpA1]
    kdT = two([128, 512], BF16)
    ke_t = two([128, 512], BF16)
    qe_t = two([128, 512], BF16)
    qb_t = two([128, 512], BF16)
    keo_t = two([128, 512], BF16)
    qeo_t = two([128, 512], BF16)
    osq1 = ar.alloc([128, 512], BF16)
    osq = [osq1, osq1]
    zq_sb = two([128, 512], BF16)
    v_tm = two([128, 4, 128], BF16)
    kd_tm = two([128, 4, 128], BF16)
    A_T = two([128, 4, 128], BF16)
    biasK = two([128, 8], F32)
    nbmid = two([128, 8], F32)
    biasD = two([128, 4], F32)
    biasKo = two([128, 4], F32)
    nb63 = two([128, 4], F32)
    ebl = two([128, 4], F32)

    s.add("pool", lambda e: e.memset(lbrawT, 0.0), w=["lbrawT"])
    s.add("sp", dma(lbrawT[0:16, :], lbg.rearrange("l (h p) -> (l h) p", p=128)), r=["lbrawT"], w=["lbrawT"], dma=True)
    s.add("pe", lambda pe: pe.transpose(out=PS[0][:, 0:128], in_=lbrawT, identity=identf[:]),
          r=["lbrawT", "identf"], w=[("ps", 0)])
    s.add("act", lambda e: e.activation(out=lbraw.rearrange("p l h -> p (l h)"), in_=PS[0][:, 0:16], func=AF.Copy),
          r=[("ps", 0)], w=["lbraw"])
    s.add("sp", dma(hgain, hgg.rearrange("o d -> d o")), w=["hgain"], dma=True)
    s.add("dve", lambda e: e.tensor_tensor(out=tmp8, in0=lbraw[:, 1, :], in1=lbraw[:, 0, :], op=ALU.subtract),
          r=["lbraw"], w=["tmp8"])
    s.add("act", lambda e: e.activation(out=tmp8, in_=tmp8, func=AF.Exp), r=["tmp8"], w=["tmp8"])
    s.add("dve", lambda e: e.tensor_scalar(out=tmp8, in0=tmp8, scalar1=1.0, scalar2=None, op0=ALU.add),
          r=["tmp8"], w=["tmp8"])
    s.add("dve", lambda e: e.reciprocal(out=lb_s, in_=tmp8), r=["tmp8"], w=["lb_s"])
    s.add("dve", lambda e: e.tensor_scalar(out=tmp8, in0=lb_s, scalar1=-1.0, scalar2=1.0, op0=ALU.mult, op1=ALU.add),
          r=["lb_s", "tmp8"], w=["tmp8"])
    s.add("act", lambda e: e.activation(out=lnoml, in_=tmp8, func=AF.Ln), r=["tmp8"], w=["lnoml"])
    s.add("pool", lambda e: e.memset(resetm, 1.0), w=["resetm"])
    s.add("pool", lambda e: e.memset(resetm.rearrange("p (c t) -> p c t", t=128)[:, :, 0:1], 0.0),
          r=["resetm"], w=["resetm"])
    s.add("pool", lambda e: e.memset(tri3, 1.0), w=["tri3"])
    s.add("pool", lambda e: e.affine_select(out=tri3, in_=tri3, pattern=[[0, 4], [1, 128]],
                                            compare_op=ALU.is_ge, fill=0.0, base=0, channel_multiplier=-1),
          r=["tri3"], w=["tri3"])
    s.add("pool", lambda e: e.affine_select(out=tri3[:, :, 64:128], in_=tri3[:, :, 64:128], pattern=[[0, 4], [1, 64]],
                                            compare_op=ALU.is_ge, fill=0.0, base=-8192, channel_multiplier=128),
          r=["tri3"], w=["tri3"])
    for q in range(2):
        s.add("pool", lambda e, q=q: e.memset(keo_t[q], 0.0), w=[("keo", q)])
        s.add("pool", lambda e, q=q: e.memset(Eqo_t[q], 0.0), w=[("Eqo", q)])

    QS = float(DH ** -0.5)
    LNQS = float(math.log(QS))
    state = {"sfi": 0, "sbi": 0}

    SEL = [set()]

    def sa(stage, *a_, **k_):
        if stage in SEL[0]:
            s.add(*a_, **k_)

    def frontA(h, p, G):
        q = G % 2
        own = G >= 4
        t0 = G * 512
        hk = hkeys(t0, 512)
        eu, L1, L2, bc = eu_t[q], L1_t[q], L2_t[q], bc_t[q]
        bz, bq = G % 2, 1 - G % 2
        sa("Pz", "pe", mmgroup(PS[bz][:], [(Wf2[p][:, kt, :], hT(kt, t0, 512)) for kt in range(8)]),
              r=[("Wf2", p)] + hk, w=[("ps", bz)])
        def vmm2(pe):
            last = None
            for i in range(4):
                tt0 = (4 * G + i) * 128
                for kt in range(8):
                    last = pe.matmul(PS[2][:, i * 128:(i + 1) * 128], hT(kt, tt0, 128), Wi2[p][:, kt, :],
                                     start=(kt == 0), stop=(kt == 7))
            return last
        sa("Pv", "pe", vmm2, r=[("Wi2", p)] + hk, w=[("ps", 2)])
        sa("Pv", "act", lambda e: e.activation(out=v_tm[q], in_=PS[2][:].rearrange("p (i d) -> p i d", i=4),
                                            func=AF.Copy), r=[("ps", 2)], w=[("v_tm", q)])
        if own:
            sa("Pq", "pe", mmgroup(PS[bq][:], [(Wq2[p][:, kt, :], hT(kt, t0, 512)) for kt in range(8)]),
                  r=[("Wq2", p)] + hk, w=[("ps", bq)])
            sa("Pq", "dve", lambda e: e.tensor_copy(out=zq_sb[q], in_=PS[bq][:]), r=[("ps", bq)], w=[("zq_sb", q)])

    def frontB(h, p, G):
        q = G % 2
        own = G >= 4
        t0 = G * 512
        hk = hkeys(t0, 512)
        eu, L1, L2, bc = eu_t[q], L1_t[q], L2_t[q], bc_t[q]
        bz = G % 2
        sa("A1", "act", lambda e: e.activation(out=eu, in_=PS[bz][:], func=AF.Exp, scale=-1.0),
              r=[("ps", bz)], w=[("eu", q)])
        sa("A1", "act", lambda e: e.activation(out=L1, in_=eu, func=AF.Ln, bias=1.0), r=[("eu", q)], w=[("L1", q)])
        sa("A1", "act", lambda e: e.activation(out=L2, in_=eu, func=AF.Ln, bias=1.0, scale=lb_s[:, h:h + 1]),
              r=[("eu", q), "lb_s"], w=[("L2", q)])
        sa("D2", "dve", lambda e: e.tensor_tensor(out=L2, in0=L2, in1=L1, op=ALU.subtract),
              r=[("L1", q), ("L2", q)], w=[("L2", q)])
        sa("D2", "dve", lambda e: e.tensor_tensor_scan(out=bc, data0=resetm, data1=L2, initial=0.0,
                                                    op0=ALU.mult, op1=ALU.add),
              r=[("L2", q), "resetm"], w=[("bc", q)])
        sa("D2", "dve", lambda e: e.tensor_tensor(out=eu, in0=PS[bz][:], in1=L1, op=ALU.add),
              r=[("ps", bz), ("L1", q), ("eu", q)], w=[("eu", q)])
        sa("D2", "dve", lambda e: e.scalar_tensor_tensor(out=eu, in0=eu, scalar=-1.0, in1=bc,
                                                      op0=ALU.mult, op1=ALU.subtract),
              r=[("eu", q), ("bc", q)], w=[("eu", q)])
        bc3 = bc.rearrange("p (i t) -> p i t", t=128)
        bc4 = bc.rearrange("p (c t) -> p c t", t=64)
        sa("D2", "dve", lambda e: e.tensor_scalar(out=biasD[q], in0=bc3[:, :, 127], scalar1=lnoml[:, h:h + 1],
                                               scalar2=None, op0=ALU.add),
              r=[("bc", q), "lnoml"], w=[("biasD", q)])
        sa("A3", "act", lambda e: e.activation(out=ebl[q], in_=bc3[:, :, 127], func=AF.Exp),
              r=[("bc", q)], w=[("ebl", q)])
        for i in range(4):
            sa("A3", "act", lambda e, i=i: e.activation(out=kdT[q][:, i * 128:(i + 1) * 128],
                                                        in_=eu[:, i * 128:(i + 1) * 128], func=AF.Exp,
                                                        bias=biasD[q][:, i:i + 1]),
               r=[("eu", q), ("biasD", q)], w=[("kdT", q, i)])
        if own:
            sa("D2", "dve", lambda e: e.tensor_scalar(out=biasK[q], in0=bc4[:, :, 31], scalar1=lnoml[:, h:h + 1],
                                                   scalar2=None, op0=ALU.add),
                  r=[("bc", q), "lnoml"], w=[("biasK", q)])
            sa("D2", "dve", lambda e: e.tensor_scalar(out=nbmid[q], in0=bc4[:, :, 31], scalar1=-1.0,
                                                      scalar2=LNQS, op0=ALU.mult, op1=ALU.add),
               r=[("bc", q)], w=[("nbmid", q)])
            sa("D2", "dve", lambda e: e.tensor_scalar(out=biasKo[q], in0=bc3[:, :, 63], scalar1=lnoml[:, h:h + 1],
                                                      scalar2=None, op0=ALU.add),
               r=[("bc", q), "lnoml"], w=[("biasKo", q)])
            sa("D2", "dve", lambda e: e.tensor_scalar(out=nb63[q], in0=bc3[:, :, 63], scalar1=-1.0,
                                                      scalar2=LNQS, op0=ALU.mult, op1=ALU.add),
               r=[("bc", q)], w=[("nb63", q)])
            eu3 = eu.rearrange("p (i t) -> p i t", t=128)
            eu4 = eu.rearrange("p (c t) -> p c t", t=64)
            for c in range(8):
                sa("A3", "act", lambda e, c=c: e.activation(out=ke_t[q][:, c * 64:(c + 1) * 64],
                                                            in_=eu[:, c * 64:(c + 1) * 64], func=AF.Exp,
                                                            bias=biasK[q][:, c:c + 1]),
                   r=[("eu", q), ("biasK", q)], w=[("ke", q, c)])
            sa("A3", "dve", lambda e: e.tensor_tensor(out=L1.rearrange("p (c t) -> p c t", t=64), in0=bc4,
                                                      in1=nbmid[q].unsqueeze(2).broadcast_to([128, 8, 64]), op=ALU.add),
               r=[("bc", q), ("nbmid", q), ("L1", q)], w=[("L1", q)])
            sa("A3", "act", lambda e: e.activation(out=Eq_t[q], in_=L1, func=AF.Exp),
               r=[("L1", q)], w=[("Eq", q, c) for c in range(8)])
            sa("A3", "dve", lambda e: e.tensor_tensor(out=L2.rearrange("p (i t) -> p i t", t=128)[:, :, 0:64],
                                                      in0=eu3[:, :, 0:64],
                                                      in1=biasKo[q].unsqueeze(2).broadcast_to([128, 4, 64]), op=ALU.add),
               r=[("eu", q), ("biasKo", q), ("L2", q)], w=[("L2", q)])
            sa("A3", "act", lambda e: e.activation(out=keo_t[q].rearrange("p (i t) -> p i t", t=128)[:, :, 0:64],
                                                   in_=L2.rearrange("p (i t) -> p i t", t=128)[:, :, 0:64], func=AF.Exp),
               r=[("L2", q), ("keo", q)], w=[("keo", q, i) for i in range(4)])
            sa("A3", "dve", lambda e: e.tensor_tensor(out=L1.rearrange("p (i t) -> p i t", t=128)[:, :, 64:128],
                                                      in0=bc3[:, :, 64:128],
                                                      in1=nb63[q].unsqueeze(2).broadcast_to([128, 4, 64]), op=ALU.add),
               r=[("bc", q), ("nb63", q), ("L1", q)], w=[("L1", q)])
            sa("A3", "act", lambda e: e.activation(out=Eqo_t[q].rearrange("p (i t) -> p i t", t=128)[:, :, 64:128],
                                                   in_=L1.rearrange("p (i t) -> p i t", t=128)[:, :, 64:128], func=AF.Exp),
               r=[("L1", q), ("Eqo", q)], w=[("Eqo", q, i) for i in range(4)])
            sa("A3", "act", lambda e: e.activation(out=Eb_t[q], in_=bc, func=AF.Exp, bias=LNQS), r=[("bc", q)], w=[("Eb", q)])
            sa("D4", "pool", lambda e: e.tensor_tensor(out=qe_t[q], in0=zq_sb[q], in1=Eq_t[q], op=ALU.mult),
               r=[("zq_sb", q)] + [("Eq", q, c) for c in range(8)], w=[("qe", q)])
            sa("D4", "pool", lambda e: e.tensor_tensor(out=qeo_t[q], in0=zq_sb[q], in1=Eqo_t[q], op=ALU.mult),
               r=[("zq_sb", q), ("Eqo", q)] + [("Eqo", q, i) for i in range(4)], w=[("qeo", q)])
            sa("D4", "pool", lambda e: e.tensor_tensor(out=qb_t[q], in0=zq_sb[q], in1=Eb_t[q], op=ALU.mult),
               r=[("zq_sb", q), ("Eb", q)], w=[("qb", q)])

    def back1(h, p, G):
        q = G % 2
        own = G >= 4
        t0 = G * 512
        hk = hkeys(t0, 512)
        def kdtr(pe):
            pv = PS[3][:].bitcast(BF16)
            last = None
            for i in range(4):
                last = pe.transpose(out=pv[:, i * 128:(i + 1) * 128], in_=kdT[q][:, i * 128:(i + 1) * 128],
                                    identity=ident[:])
            return last
        sa("R1", "pe", kdtr, r=[("kdT", q, i) for i in range(4)] + ["ident"], w=[("ps", 3)])
        sa("R1", "dve", lambda e: e.tensor_copy(out=kd_tm[q], in_=PS[3][:].bitcast(BF16)[:, 0:512].rearrange(
            "p (c k) -> p c k", c=4)), r=[("ps", 3)], w=[("kd_tm", q)])

        if own:
            def amm(pe):
                last = None
                for i in range(4):
                    last = pe.matmul(PS[4][:, i * 128:(i + 1) * 128], ke_t[q][:, i * 128:(i + 1) * 128],
                                     qe_t[q][:, i * 128:(i + 1) * 128], start=True, stop=True)
                for i in range(4):
                    last = pe.matmul(PS[7][:, i * 128:(i + 1) * 128], keo_t[q][:, i * 128:(i + 1) * 128],
                                     qeo_t[q][:, i * 128:(i + 1) * 128], start=True, stop=True)
                return last
            sa("R1", "pe", amm, r=[("ke", q, c) for c in range(8)] + [("keo", q, i) for i in range(4)] +
                  [("qe", q), ("qeo", q), ("keo", q)], w=[("ps", 4), ("ps", 7)])
            sa("R2", "dve", lambda e: e.tensor_tensor(out=tmpA[q].rearrange("p (i t) -> p i t", i=4),
                                                   in0=PS[4][:].rearrange("p (i t) -> p i t", i=4),
                                                   in1=tri3, op=ALU.mult),
                  r=[("ps", 4), "tri3"], w=["tmpA"])
            sa("R2", "dve", lambda e: e.tensor_tensor(out=A_T[q].rearrange("p i t -> p (i t)"), in0=tmpA[q], in1=PS[7][:],
                                                   op=ALU.add),
                  r=["tmpA", ("ps", 7)], w=[("A_T", q)])

        if "CH" not in SEL[0]:
            return
        if G == 0:
            sfi0 = state["sfi"]
            sa("CH", "pool", lambda e: e.memset(S_f[sfi0], 0.0), w=[("S_f", sfi0)])

        def snew(pe):
            last = None
            for i in range(4):
                last = pe.matmul(PS[6][:, i * 128:(i + 1) * 128], kd_tm[q][:, i, :], v_tm[q][:, i, :],
                                 start=True, stop=True)
            return last
        sa("CH", "pe", snew, r=[("kd_tm", q), ("v_tm", q)], w=[("ps", 6)])
        for i in range(4):
            sfi, sbi = state["sfi"], state["sbi"]
            if own:
                def omm(pe, i=i, sbi=sbi):
                    pe.matmul(PS[5][:, i * 128:(i + 1) * 128], S_bf[sbi], qb_t[q][:, i * 128:(i + 1) * 128],
                              start=True, stop=False)
                    return pe.matmul(PS[5][:, i * 128:(i + 1) * 128], v_tm[q][:, i, :], A_T[q][:, i, :],
                                     start=False, stop=True)
                sa("CH", "pe", omm, r=[("S_bf", sbi), ("qb", q), ("v_tm", q), ("A_T", q)], w=[("ps", 5)])
            nsf = 1 - sfi
            sa("CH", "dve", lambda e, i=i, sfi=sfi, nsf=nsf: e.scalar_tensor_tensor(
                out=S_f[nsf], in0=S_f[sfi], scalar=ebl[q][:, i:i + 1],
                in1=PS[6][:, i * 128:(i + 1) * 128], op0=ALU.mult, op1=ALU.add),
                r=[("S_f", sfi), ("ebl", q), ("ps", 6)], w=[("S_f", nsf)])
            state["sfi"] = nsf
            need_cast = (G == 3 and i == 3) or (G >= 4 and not (G == 7 and i == 3))
            if need_cast:
                nsb = 1 - sbi
                state["sbi"] = nsb
                sa("CH", "act", lambda e, nsb=nsb, nsf=nsf: e.activation(out=S_bf[nsb], in_=S_f[nsf], func=AF.Copy),
                      r=[("S_f", nsf)], w=[("S_bf", nsb)])

    def back2(h, p, G):
        q = G % 2
        own = G >= 4
        t0 = G * 512
        hk = hkeys(t0, 512)
        if own:
            g = G - 4
            rs, sg, m1 = rs_t[q], sg_t[q], m1_t[q]
            sa("Na", "act", lambda e: e.activation(out=osq[q], in_=PS[5][:], func=AF.Square), r=[("ps", 5)], w=["osq"])
            sa("Na", "pe", lambda pe: pe.matmul(PS[4][:], ones_bf[:], osq[q], start=True, stop=True),
                  r=["osq", "ones_bf"], w=[("ps", 4)])
            sa("Na", "act", lambda e: e.activation(out=rs, in_=PS[4][:], func=AF.Ln, scale=1.0 / DH, bias=EPS),
                  r=[("ps", 4)], w=[("rs", q)])
            sa("Na", "act", lambda e: e.activation(out=rs, in_=rs, func=AF.Exp, scale=-0.5), r=[("rs", q)], w=[("rs", q)])
            sa("Na", "dve", lambda e: e.scalar_tensor_tensor(out=m1, in0=PS[5][:], scalar=hgain, in1=rs,
                                                          op0=ALU.mult, op1=ALU.mult),
                  r=[("ps", 5), ("rs", q), "hgain"], w=[("m1", q)])
            sa("ZG", "pe", mmgroup(PS[7][:], [(Wg2[p][:, kt, :], hT(kt, t0, 512)) for kt in range(8)]),
                  r=[("Wg2", p)] + hk, w=[("ps", 7)])
            sa("Na", "act", lambda e: e.activation(out=sg, in_=PS[7][:], func=AF.Exp, scale=-1.0),
                  r=[("ps", 7)], w=[("sg", q)])
            sa("Na", "act", lambda e: e.activation(out=sg, in_=sg, func=AF.Ln, bias=1.0), r=[("sg", q)], w=[("sg", q)])
            sa("Na", "act", lambda e: e.activation(out=sg, in_=sg, func=AF.Exp, scale=-1.0), r=[("sg", q)], w=[("sg", q)])
            sa("Nb", "dve", lambda e: e.tensor_tensor(out=sg, in0=PS[7][:], in1=sg, op=ALU.mult),
                  r=[("ps", 7), ("sg", q)], w=[("sg", q)])
            sa("Nb", "pool", lambda e: e.tensor_tensor(out=ohgT[:, h, g * 512:(g + 1) * 512], in0=m1, in1=sg, op=ALU.mult),
                  r=[("m1", q), ("sg", q)], w=[("ohgT", h, g)])

    items = [(hi_, h, G) for hi_, h in enumerate(heads) for G in range(8)]

    def wload(hi_):
        h = heads[hi_]
        p = hi_ % 2
        for (W, c0_, nm) in ((Wf2, C_HF, "Wf2"), (Wi2, C_HI, "Wi2"), (Wq2, C_HQ, "Wq2"), (Wg2, C_HG, "Wg2")):
            s.add("pool", dma(W[p], w_in_v[:, :, c0_ + h * 128:c0_ + (h + 1) * 128]), w=[(nm, p)], dma=True)

    def run(fn, item, *stages):
        if item is None:
            return
        hi_, h, G = item
        SEL[0] = set(stages)
        fn(h, hi_ % 2, G)

    wload(0)
    if len(heads) > 1:
        wload(1)
    run(frontA, items[0], "Pz", "Pv", "Pq")
    for st in ("A1", "D2", "A3", "D4"):
        run(frontB, items[0], st)
    run(frontA, items[1], "Pz")
    for k, it in enumerate(items):
        nx = items[k + 1] if k + 1 < len(items) else None
        nx2 = items[k + 2] if k + 2 < len(items) else None
        if it[2] == 1 and it[0] >= 1 and it[0] + 1 < len(heads):
            wload(it[0] + 1)
        run(frontB, nx, "A1")
        run(back1, it, "R1")
        run(back1, it, "R2")
        run(frontA, nx, "Pv")
        run(frontB, nx, "D2")
        run(back2, it, "ZG")
        run(back1, it, "CH")
        run(frontB, nx, "A3")
        run(frontA, nx, "Pq")
        run(back2, it, "Na")
        run(frontB, nx, "D4")
        run(back2, it, "Nb")
        run(frontA, nx2, "Pz")

    if "ohgT" in tap_aps:
        for hh in heads:
            s.add("sp", dma(tap_aps["ohgT"][:, hh, :], ohgT[:, hh, :]), r=[("ohgT", hh, g) for g in range(4)], dma=True)
    if stop_after == "p2":
        return finish()

    s.barrier()
    ar.reset()
    mergedT = ar.alloc([128, 8, NOWN], BF16)
    off_after_merged = ar.off
    ar2 = Arena(hT_pre[:].rearrange("p k t -> p (k t)").bitcast(F32), 8 * NPRE // 2)
    Wo = ar2.alloc([128, 8, D], BF16)
    W4 = {nm: [ar.alloc([128, 8, 128], BF16) for _ in range(2)] for nm in ("bh", "ba", "g0", "g1")}
    th0 = [ar.alloc([128, 512], F32) for _ in range(2)]
    th1 = [ar.alloc([128, 512], F32) for _ in range(2)]
    ma_t = [ar.alloc([128, 512], F32) for _ in range(2)]
    mb_t = [ar.alloc([128, 512], F32) for _ in range(2)]
    w_bh_v = w_bh.rearrange("(kt p) c -> p kt c", p=128)
    w_ba_v = w_ba.rearrange("(kt p) c -> p kt c", p=128)
    it = 0

    def w4load(ct):
        p = ct % 2
        s.add("pool", dma(W4["bh"][p], w_bh_v[:, :, ct * 128:(ct + 1) * 128]), w=[("W4bh", p)], dma=True)
        s.add("pool", dma(W4["ba"][p], w_ba_v[:, :, ct * 128:(ct + 1) * 128]), w=[("W4ba", p)], dma=True)
        s.add("pool", dma(W4["g0"][p], w_in_v[:, :, C_GL + ct * 128:C_GL + (ct + 1) * 128]), w=[("W4g0", p)], dma=True)
        s.add("pool", dma(W4["g1"][p], w_in_v[:, :, C_GL + D + ct * 128:C_GL + D + (ct + 1) * 128]),
              w=[("W4g1", p)], dma=True)
    w4load(0)
    w4load(1)
    for ct in range(8):
        p = ct % 2
        for g in range(4):
            q = it % 2
            it += 1
            b0 = 4 * q
            cols = slice(g * 512, (g + 1) * 512)
            s.add("pe", mmgroup(PS[b0][:], [(W4["bh"][p][:, kt, :], ohgT[:, kt, cols]) for kt in range(8)]),
                  r=[("W4bh", p)] + [("ohgT", kt, g) for kt in range(8)], w=[("ps", b0)])
            s.add("pe", mmgroup(PS[b0 + 1][:], [(W4["ba"][p][:, kt, :], attT[:, kt, cols]) for kt in range(8)]),
                  r=[("W4ba", p)] + [("attT", kt, g) for kt in range(8)], w=[("ps", b0 + 1)])
            s.add("pe", mmgroup(PS[b0 + 2][:], [(W4["g0"][p][:, kt, :], hT_own[:, kt, cols]) for kt in range(8)]),
                  r=[("W4g0", p)] + hkeys(NPRE + g * 512, 512), w=[("ps", b0 + 2)])
            s.add("pe", mmgroup(PS[b0 + 3][:], [(W4["g1"][p][:, kt, :], hT_own[:, kt, cols]) for kt in range(8)]),
                  r=[("W4g1", p)] + hkeys(NPRE + g * 512, 512), w=[("ps", b0 + 3)])
            s.add("act", lambda e, q=q, b0=b0: e.activation(out=th0[q], in_=PS[b0 + 2][:], func=AF.Tanh, scale=0.5),
                  r=[("ps", b0 + 2)], w=[("th0", q)])
            s.add("act", lambda e, q=q, b0=b0: e.activation(out=th1[q], in_=PS[b0 + 3][:], func=AF.Tanh, scale=0.5),
                  r=[("ps", b0 + 3)], w=[("th1", q)])
            s.add("dve", lambda e, q=q, b0=b0: e.scalar_tensor_tensor(out=ma_t[q], in0=th0[q], scalar=1.0,
                                                                      in1=PS[b0][:], op0=ALU.add, op1=ALU.mult),
                  r=[("th0", q), ("ps", b0)], w=[("ma", q)])
            s.add("dve", lambda e, q=q, b0=b0: e.scalar_tensor_tensor(out=mb_t[q], in0=th1[q], scalar=1.0,
                                                                      in1=PS[b0 + 1][:], op0=ALU.add, op1=ALU.mult),
                  r=[("th1", q), ("ps", b0 + 1)], w=[("mb", q)])
            s.add("dve", lambda e, q=q, ct=ct, cols=cols: e.tensor_tensor(out=mergedT[:, ct, cols], in0=ma_t[q],
                                                                         in1=mb_t[q], op=ALU.add),
                  r=[("ma", q), ("mb", q)], w=[("mergedT", ct, g)])
            if g == 0 and ct >= 1 and ct + 1 < 8:
                w4load(ct + 1)
            if g == 2 and ct == 5:
                s.add("pool", dma(Wo, w_out.rearrange("(kt p) c -> p kt c", p=128)), w=["Wo"], dma=True)

    if "mergedT" in tap_aps:
        for ct in range(8):
            for g in range(4):
                s.add("sp", dma(tap_aps["mergedT"][:, ct, g * 512:(g + 1) * 512], mergedT[:, ct, g * 512:(g + 1) * 512]),
                      r=[("mergedT", ct, g)], dma=True)
    if stop_after == "p4a":
        return finish()

    s.barrier()
    ar.off = off_after_merged
    gB2 = ar2.alloc([128, D], F32)
    xn2 = [ar2.alloc([128, D], BF16) for _ in range(3)]
    junk2 = ar2.alloc([128, D], BF16)
    ssq2 = ar2.alloc([128, 16], F32)
    rstd2 = ar2.alloc([128, 16], F32)
    xres = [ar.alloc([128, D], F32) for _ in range(3)]
    x1t = [ar.alloc([128, D], F32) for _ in range(3)]
    s.add("sp", dma(gB2, ffn_g.partition_broadcast(128)), w=["gB"], dma=True)
    h2T = hT_own
    fins = []
    def xres_load(tt):
        s.add("sp", dma(xres[tt % 3], x_own[tt * 128:(tt + 1) * 128, :]), w=[("xres", tt % 3)], dma=True)
    xres_load(0)
    xres_load(1)
    for tt in range(16):
        b = tt % 3
        if tt + 2 < 16:
            xres_load(tt + 2)
        for ch in range(2):
            bank = 2 + 2 * b + ch
            s.add("pe", mmgroup(PS[bank][:], [(mergedT[:, ct, tt * 128:(tt + 1) * 128], Wo[:, ct, ch * 512:(ch + 1) * 512])
                                              for ct in range(8)]),
                  r=["Wo"] + [("mergedT", ct, tt // 4) for ct in range(8)], w=[("ps", bank)])
            s.add("dve", lambda e, b=b, ch=ch, bank=bank: e.scalar_tensor_tensor(
                out=x1t[b][:, ch * 512:(ch + 1) * 512], in0=PS[bank][:], scalar=0.5,
                in1=xres[b][:, ch * 512:(ch + 1) * 512], op0=ALU.mult, op1=ALU.add),
                r=[("ps", bank), ("xres", b)], w=[("x1t", b, ch)])
        s.add("sp", dma(out[tt * 128:(tt + 1) * 128, :], x1t[b]), r=[("x1t", b, 0), ("x1t", b, 1)],
              w=[("out", tt)], dma=True)
        fins.append(norm_transpose(x1t[b], [("x1t", b, 0), ("x1t", b, 1)], gB2, ssq2[:, tt:tt + 1],
                                   rstd2[:, tt:tt + 1], xn2[b], ("xn", b), junk2, tt % 2,
                                   h2T[:, :, tt * 128:(tt + 1) * 128], ("h2T", tt), 100 + tt, defer_tr=True))
        if len(fins) > 1:
            fins.pop(0)()
    while fins:
        fins.pop(0)()

    if "h2T" in tap_aps:
        for kt in range(8):
            for g in range(4):
                s.add("sp", dma(tap_aps["h2T"][:, kt, g * 512:(g + 1) * 512], h2T[:, kt, g * 512:(g + 1) * 512]),
                      r=[("h2T", t) for t in range(16)], dma=True)
    if stop_after == "p4b":
        return finish()

    s.barrier()
    ar.reset()
    aT_x = ar.alloc([128, 6, NOWN], BF16)

    def aT(ft):
        if ft < 8:
            return ohgT[:, ft, :]
        if ft < 16:
            return attT[:, ft - 8, :]
        return aT_x[:, ft - 16, :]
    Wg5 = [ar.alloc([128, 8, 128], BF16) for _ in range(2)]
    Wu5 = [ar.alloc([128, 8, 128], BF16) for _ in range(2)]
    Wd5 = [ar.alloc([128, NFT, 512], BF16) for _ in range(2)]
    ar2.reset()
    th5 = [ar2.alloc([128, 512], F32) for _ in range(2)]
    sg5 = [ar2.alloc([128, 512], F32) for _ in range(2)]
    x1r = [ar2.alloc([128, 512], F32) for _ in range(3)]
    o5 = [ar2.alloc([128, 512], F32) for _ in range(3)]
    w_fg_v = w_fg.rearrange("(kt p) c -> p kt c", p=128)
    w_fu_v = w_fu.rearrange("(kt p) c -> p kt c", p=128)
    w_fd_v = w_fd.rearrange("(ft p) c -> p ft c", p=128)
    it = 0
    for ft in range(NFT):
        p = ft % 2
        s.add("pool", dma(Wg5[p], w_fg_v[:, :, ft * 128:(ft + 1) * 128]), w=[("Wg5", p)], dma=True)
        s.add("pool", dma(Wu5[p], w_fu_v[:, :, ft * 128:(ft + 1) * 128]), w=[("Wu5", p)], dma=True)
        for g in range(4):
            q = it % 2
            it += 1
            b0 = 2 * q
            cols = slice(g * 512, (g + 1) * 512)
            s.add("pe", mmgroup(PS[b0][:], [(Wg5[p][:, kt, :], h2T[:, kt, cols]) for kt in range(8)]),
                  r=[("Wg5", p)] + [("h2T", t) for t in range(4 * g, 4 * g + 4)], w=[("ps", b0)])
            s.add("pe", mmgroup(PS[b0 + 1][:], [(Wu5[p][:, kt, :], h2T[:, kt, cols]) for kt in range(8)]),
                  r=[("Wu5", p)] + [("h2T", t) for t in range(4 * g, 4 * g + 4)], w=[("ps", b0 + 1)])
            s.add("act", lambda e, q=q, b0=b0: e.activation(out=th5[q], in_=PS[b0][:], func=AF.Tanh, scale=0.5),
                  r=[("ps", b0)], w=[("th5", q)])
            s.add("dve", lambda e, q=q, b0=b0: e.scalar_tensor_tensor(out=sg5[q], in0=th5[q], scalar=1.0,
                                                                      in1=PS[b0][:], op0=ALU.add, op1=ALU.mult),
                  r=[("th5", q), ("ps", b0)], w=[("sg5", q)])
            s.add("dve", lambda e, q=q, b0=b0, ft=ft, cols=cols: e.tensor_tensor(out=aT(ft)[:, cols], in0=sg5[q],
                                                                                in1=PS[b0 + 1][:], op=ALU.mult),
                  r=[("sg5", q), ("ps", b0 + 1)], w=[("aT", ft, g)])
    for ch in range(2):
        s.add("pool", dma(Wd5[ch], w_fd_v[:, :, ch * 512:(ch + 1) * 512]), w=[("Wd5", ch)], dma=True)
    def x1r_load(it_):
        ch_, tt_ = it_ // 16, it_ % 16
        s.add("sp", dma(x1r[it_ % 3], out[tt_ * 128:(tt_ + 1) * 128, ch_ * 512:(ch_ + 1) * 512]),
              r=[("out", tt_)], w=[("x1r", it_ % 3)], dma=True)
    x1r_load(0)
    x1r_load(1)
    it = 0
    for ch in range(2):
        for tt in range(16):
            q = it % 3
            bank = 4 + it % 2
            if it + 2 < 32:
                x1r_load(it + 2)
            it += 1
            s.add("pe", mmgroup(PS[bank][:], [(aT(ft)[:, tt * 128:(tt + 1) * 128], Wd5[ch][:, ft, :])
                                              for ft in range(NFT)]),
                  r=[("Wd5", ch)] + [("aT", ft, tt // 4) for ft in range(NFT)], w=[("ps", bank)])
            s.add("dve", lambda e, q=q, bank=bank: e.scalar_tensor_tensor(out=o5[q], in0=PS[bank][:], scalar=0.5,
                                                                          in1=x1r[q], op0=ALU.mult, op1=ALU.add),
                  r=[("ps", bank), ("x1r", q)], w=[("o5", q)])
            s.add("sp", dma(out[tt * 128:(tt + 1) * 128, ch * 512:(ch + 1) * 512], o5[q]),
                  r=[("o5", q), ("out", tt)] if ch == 1 else [("o5", q)],
                  w=[("outf", tt, ch)], dma=True)
    return finish()


def t5_bucket_np(dist):
    n = np.maximum(dist, 0)
    nf = np.maximum(n, 1).astype(np.float32)
    large = 16 + (np.log(nf / np.float32(16)) / np.float32(np.log(8.0)) * np.float32(16)).astype(np.int32)
    large = np.minimum(large, 31)
    return np.where(n < 16, n, large)


def host_constants(rel_bias_table, half):
    s_ = np.arange(128)[:, None]
    q_ = np.arange(128)[None, :]
    d_diag = q_ - s_
    d_off = 128 + q_ - s_
    tab = np.asarray(rel_bias_table, np.float32)
    btab = np.empty((H, 128, 2, 128), np.float32)
    bd = tab[t5_bucket_np(d_diag)]
    bo = tab[t5_bucket_np(d_off)]
    for h in range(H):
        btab[h, :, 0, :] = np.where(d_diag >= 0, bd[:, :, h], np.float32(NEG))
        btab[h, :, 1, :] = bo[:, :, h]
    vmask = np.zeros((128, 16, 16), np.float32)
    for qt in range(16):
        cur = 8 + qt // 2
        for n in range(16):
            if n >= cur or (half == 0 and n < 8):
                vmask[:, qt, n] = NEG
    onehot = np.zeros((16, 16 * 128), np.float32)
    for n in range(16):
        onehot[n, n * 128:(n + 1) * 128] = 1.0
    return btab, vmask, (vmask + np.float32(NEG)).astype(np.float32), onehot


def make_in_maps(inputs):
    x = np.asarray(inputs["x"], np.float32)
    shared = {
        "attn_norm_g": np.ascontiguousarray(inputs["attn_norm_g"], np.float32).reshape(1, D),
        "w_in": np.ascontiguousarray(np.asarray(inputs["w_in"], np.float32)[0]),
        "hg_lb_gamma": np.ascontiguousarray(inputs["hg_lb_gamma"], np.float32),
        "hg_out_norm_g": np.ascontiguousarray(inputs["hg_out_norm_g"], np.float32).reshape(1, DH),
        "q_norm_g": np.ascontiguousarray(inputs["q_norm_g"], np.float32).reshape(1, DH),
        "k_norm_g": np.ascontiguousarray(inputs["k_norm_g"], np.float32).reshape(1, DH),
        "rel_bias_table": np.ascontiguousarray(inputs["rel_bias_table"], np.float32),
        "w_branch_hg": np.ascontiguousarray(np.asarray(inputs["w_branch_hg"], np.float32)[0]),
        "w_branch_attn": np.ascontiguousarray(np.asarray(inputs["w_branch_attn"], np.float32)[0]),
        "w_out": np.ascontiguousarray(np.asarray(inputs["w_out"], np.float32)[0]),
        "ffn_norm_g": np.ascontiguousarray(inputs["ffn_norm_g"], np.float32).reshape(1, D),
        "w_ffn_gate": np.ascontiguousarray(np.asarray(inputs["w_ffn_gate"], np.float32)[0]),
        "w_ffn_up": np.ascontiguousarray(np.asarray(inputs["w_ffn_up"], np.float32)[0]),
        "w_ffn_down": np.ascontiguousarray(np.asarray(inputs["w_ffn_down"], np.float32)[0]),
    }
    consts = [host_constants(inputs["rel_bias_table"], half) for half in range(2)]
    in_maps = []
    for c in range(8):
        b, half = c // 2, c % 2
        if half == 0:
            x_pre = np.zeros((NPRE, D), np.float32)
            x_own = np.ascontiguousarray(x[b, :NOWN])
        else:
            x_pre = np.ascontiguousarray(x[b, :NPRE])
            x_own = np.ascontiguousarray(x[b, NPRE:])
        btab, vmask, vmask2, onehot = consts[half]
        m = dict(shared)
        m.update({"x_pre": x_pre, "x_own": x_own, "vmask": vmask, "vmask2": vmask2, "onehot": onehot, "btab": btab})
        in_maps.append(m)
    return in_maps


def kernel(**inputs):
    nc = build_program()
    in_maps = make_in_maps(inputs)
    res = run_bass_kernel_spmd(nc, in_maps, core_ids=list(range(8)))
    outp = np.empty((4, NTOK, D), np.float32)
    for c in range(8):
        b, half = c // 2, c % 2
        outp[b, half * NOWN:(half + 1) * NOWN] = np.asarray(res.results[c]["out"], np.float32)
    return outp
```

```python
import contextlib
import math
import numpy as np
import concourse.bass as bass
import concourse.mybir as mybir
from concourse.bass_utils import run_bass_kernel_spmd

F32 = mybir.dt.float32
BF16 = mybir.dt.bfloat16
AF = mybir.ActivationFunctionType
ALU = mybir.AluOpType
AX = mybir.AxisListType

D = 1024
NPRE = 2048
NOWN = 2048
NTOK = NPRE + NOWN
H = 8
DH = 128
DFF = 2816
NFT = DFF // 128
EPS = 1e-6
NEG = -30000.0
C_HQ, C_HF, C_HI, C_HG, C_AQ, C_AK, C_AV, C_GL = 0, 1024, 2048, 3072, 4096, 5120, 6144, 7168
INC = 9216
ARENA_WORDS = 20096


class Sched:
    CAP = 30000
    NDMA = 8

    def __init__(self):
        self.ops = []

    def add(self, eng, fn, r=(), w=(), dma=False):
        self.ops.append(dict(eng=eng, fn=fn, r=tuple(r), w=tuple(w), dma=dma))

    def barrier(self):
        self.ops.append(None)

    def emit(self, nc, stack):
        ops = self.ops
        engs = ["pe", "act", "dve", "pool", "sp"]
        last_w = {}
        readers = {}
        cur_last = {}
        cur_dmas = []
        pending = {}
        for i, op in enumerate(ops):
            if op is None:
                B = set(cur_last.values()) | set(cur_dmas)
                for e in engs:
                    pending[e] = pending.get(e, set()) | B
                cur_dmas = []
                continue
            deps = set()
            for k in op["r"]:
                if k in last_w:
                    deps.add(last_w[k])
            for k in op["w"]:
                if k in last_w:
                    deps.add(last_w[k])
                for j in readers.get(k, ()):
                    deps.add(j)
            e = op["eng"]
            if pending.get(e):
                deps |= pending.pop(e)
            deps.discard(i)
            op["deps"] = deps
            for k in op["r"]:
                readers.setdefault(k, []).append(i)
            for k in op["w"]:
                last_w[k] = i
                readers[k] = []
            if op["dma"]:
                cur_dmas.append(i)
            else:
                cur_last[e] = i
        cnt = {e: 0 for e in engs}
        dcnt = {e: 0 for e in engs}
        sems = {}

        def get_sem(name):
            if name not in sems:
                sems[name] = stack.enter_context(nc.semaphore(name))
            return sems[name]

        for op in ops:
            if op is None:
                continue
            e = op["eng"]
            if op["dma"]:
                j = dcnt[e]
                dcnt[e] += 1
                slot = j % self.NDMA
                op["sig"] = (get_sem(f"d_{e}_{slot}"), 16 * (j // self.NDMA + 1))
                op["prev"] = (get_sem(f"d_{e}_{slot}"), 16 * (j // self.NDMA))
            else:
                k = cnt[e]
                cnt[e] += 1
                op["sig"] = (get_sem(f"c_{e}_{k // self.CAP}"), (k % self.CAP) + 1)
        per_eng = {e: [] for e in engs}
        for i, op in enumerate(ops):
            if op is not None:
                per_eng[op["eng"]].append(i)
        self.stats = {e: len(per_eng[e]) for e in engs}

        block = stack.enter_context(nc.Block())

        def run_engine(eng_name):
            def body(eng):
                waited = {}
                for i in per_eng[eng_name]:
                    op = ops[i]
                    waits = []
                    for d in sorted(op["deps"]):
                        dop = ops[d]
                        if dop["eng"] == "pe" and eng_name == "pe" and not dop["dma"]:
                            continue
                        waits.append(dop["sig"])
                    if op["dma"] and op["prev"][1] > 0:
                        waits.append(op["prev"])
                    for sem, val in waits:
                        key = id(sem)
                        if waited.get(key, 0) >= val:
                            continue
                        waited[key] = val
                        eng.wait_ge(sem, val)
                    ins = op["fn"](eng)
                    sem, val = op["sig"]
                    ins.then_inc(sem, 16 if op["dma"] else 1)
                for slot in range(self.NDMA):
                    name = f"d_{eng_name}_{slot}"
                    if name in sems:
                        n = (dcnt[eng_name] - slot + self.NDMA - 1) // self.NDMA
                        if n > 0:
                            eng.wait_ge(sems[name], 16 * n)
            return body

        block.sync(run_engine("sp"))
        block.tensor(run_engine("pe"))
        block.scalar(run_engine("act"))
        block.vector(run_engine("dve"))
        block.gpsimd(run_engine("pool"))


class Arena:
    def __init__(self, t, words):
        self.t = t
        self.words = words
        self.off = 0

    def reset(self):
        self.off = 0

    def alloc(self, shape, dt):
        n = 1
        for d_ in shape[1:]:
            n *= d_
        w = (n + 1) // 2 if dt == BF16 else n
        w = (w + 7) // 8 * 8
        assert self.off + w <= self.words, ("arena overflow", self.off, w, self.words)
        ap = self.t[:, self.off:self.off + w]
        self.off += w
        if dt == BF16:
            ap = ap.bitcast(BF16)
        ap = ap[:, 0:n]
        if len(shape) == 3:
            ap = ap.rearrange("p (a b) -> p a b", a=shape[1])
        elif len(shape) == 4:
            ap = ap.rearrange("p (a b c) -> p a b c", a=shape[1], b=shape[2])
        if shape[0] != 128:
            ap = ap[0:shape[0]]
        return ap


def build_program(stop_after=None, taps=(), heads=None):
    nc = bass.Bass("TRN2", target_bir_lowering=False)
    s = Sched()
    stack = contextlib.ExitStack()
    heads = list(range(H)) if heads is None else list(heads)

    def din(name, shape, dt=F32):
        return nc.dram_tensor(name, list(shape), dt, kind="ExternalInput").ap()

    def dout(name, shape, dt=F32):
        return nc.dram_tensor(name, list(shape), dt, kind="ExternalOutput").ap()

    x_pre = din("x_pre", [NPRE, D])
    x_own = din("x_own", [NOWN, D])
    attn_g = din("attn_norm_g", [1, D])
    w_in = din("w_in", [D, INC])
    lbg = din("hg_lb_gamma", [2, D])
    hgg = din("hg_out_norm_g", [1, DH])
    qg = din("q_norm_g", [1, DH])
    kg = din("k_norm_g", [1, DH])
    relb = din("rel_bias_table", [32, H])
    w_bh = din("w_branch_hg", [D, D])
    w_ba = din("w_branch_attn", [D, D])
    w_out = din("w_out", [D, D])
    ffn_g = din("ffn_norm_g", [1, D])
    w_fg = din("w_ffn_gate", [D, DFF])
    w_fu = din("w_ffn_up", [D, DFF])
    w_fd = din("w_ffn_down", [DFF, D])
    vm_in = din("vmask", [128, 16, 16])
    vm2_in = din("vmask2", [128, 16, 16])
    oneh_in = din("onehot", [16, 16 * 128])
    btab_in = din("btab", [H, 128, 2, 128])
    out = dout("out", [NOWN, D])
    tap_aps = {}
    for name, shape, dt in taps:
        tap_aps[name] = dout("tap_" + name, shape, dt)

    def sb(name, shape, dt):
        return stack.enter_context(nc.sbuf_tensor(name, list(shape), dt))

    PS = [stack.enter_context(nc.psum_tensor(f"ps{i}", [128, 512], F32)) for i in range(8)]

    def dma(o, i):
        return lambda e: e.dma_start(out=o, in_=i)

    def mmgroup(out_ap, pairs):
        def fn(pe):
            last = None
            n = len(pairs)
            for i, (l, r_) in enumerate(pairs):
                last = pe.matmul(out_ap, l, r_, start=(i == 0), stop=(i == n - 1))
            return last
        return fn

    def finish():
        s.emit(nc, stack)
        stack.close()
        build_program.stats = s.stats
        return nc

    ident = sb("ident", [128, 128], BF16)
    ones_bf = sb("ones_bf", [128, 128], BF16)
    identf = sb("identf", [128, 128], F32)
    neghalf = sb("neghalf", [128, 1], F32)
    hT_own = sb("hT_own", [128, 8, NOWN], BF16)
    hT_pre = sb("hT_pre", [128, 8, NPRE], BF16)
    ohgT = sb("ohgT", [128, H, NOWN], BF16)
    attT = sb("attT", [128, H, NOWN], BF16)
    arena_t = sb("arena", [128, ARENA_WORDS], F32)
    ar = Arena(arena_t, ARENA_WORDS)

    s.add("pool", lambda e: e.memset(identf[:], 0.0), w=["identf"])
    s.add("pool", lambda e: e.affine_select(out=identf[:], in_=identf[:], pattern=[[-1, 128]],
                                            compare_op=ALU.not_equal, fill=1.0, base=0,
                                            channel_multiplier=1), r=["identf"], w=["identf"])
    s.add("pool", lambda e: e.tensor_copy(out=ident[:], in_=identf[:]), r=["identf"], w=["ident"])
    s.add("pool", lambda e: e.memset(ones_bf[:], 1.0), w=["ones_bf"])
    s.add("pool", lambda e: e.memset(neghalf[:], -0.5), w=["neghalf"])

    def hT(kt, t0, n):
        if t0 < NPRE:
            return hT_pre[:, kt, t0:t0 + n]
        return hT_own[:, kt, t0 - NPRE:t0 - NPRE + n]

    def hT3(t0, n):
        if t0 < NPRE:
            return hT_pre[:, :, t0:t0 + n]
        return hT_own[:, :, t0 - NPRE:t0 - NPRE + n]

    def hkeys(t0, n):
        return [("hT", t) for t in range(t0 // 128, (t0 + n) // 128)]

    def norm_transpose(xtile, xkey, gBt, ssq_col, rstd_col, xn_t, xnkey, junk_t, bank, dst3, dkey, skey, defer_tr=False,
                       copy_eng="act"):
        s.add("act", lambda e: e.activation(out=junk_t, in_=xtile, func=AF.Square, accum_out=ssq_col),
              r=list(xkey), w=["junk", ("ssq", skey)])
        s.add("act", lambda e: e.activation(out=rstd_col, in_=ssq_col, func=AF.Ln, scale=1.0 / D, bias=EPS),
              r=[("ssq", skey)], w=[("rstd1", skey)])
        s.add("act", lambda e: e.activation(out=rstd_col, in_=rstd_col, func=AF.Exp, scale=-0.5),
              r=[("rstd1", skey)], w=[("rstd1", skey)])
        s.add("dve", lambda e: e.scalar_tensor_tensor(out=xn_t, in0=xtile, scalar=rstd_col, in1=gBt,
                                                      op0=ALU.mult, op1=ALU.mult),
              r=list(xkey) + [("rstd1", skey), "gB"], w=[xnkey])

        def tr(pe):
            pv = PS[bank][:].bitcast(BF16)
            last = None
            for kt in range(8):
                last = pe.transpose(out=pv[:, kt * 128:(kt + 1) * 128], in_=xn_t[:, kt * 128:(kt + 1) * 128],
                                    identity=ident[:])
            return last
        if not defer_tr:
            s.add("pe", tr, r=[xnkey, "ident"], w=[("ps", bank)])

        def fin():
            if defer_tr:
                s.add("pe", tr, r=[xnkey, "ident"], w=[("ps", bank)])
            pv3 = PS[bank][:].bitcast(BF16)[:, 0:1024].rearrange("p (k t) -> p k t", k=8)
            if copy_eng == "dve":
                s.add("dve", lambda e: e.tensor_copy(out=dst3, in_=pv3), r=[("ps", bank)], w=[dkey])
            else:
                s.add("act", lambda e: e.activation(out=dst3, in_=pv3, func=AF.Copy), r=[("ps", bank)], w=[dkey])
        return fin

    ar.reset()
    gB = ar.alloc([128, D], F32)
    xt = [ar.alloc([128, D], F32) for _ in range(4)]
    xn = [ar.alloc([128, D], BF16) for _ in range(4)]
    junk = ar.alloc([128, D], BF16)
    ssq = ar.alloc([128, 32], F32)
    rstd1 = ar.alloc([128, 32], F32)
    s.add("sp", dma(gB, attn_g.partition_broadcast(128)), w=["gB"], dma=True)
    fins = []
    for ti in range(32):
        src = x_pre if ti < 16 else x_own
        r0 = (ti % 16) * 128
        b = ti % 4
        s.add("sp", dma(xt[b], src[r0:r0 + 128, :]), w=[("xt", b)], dma=True)
        fins.append(norm_transpose(xt[b], [("xt", b)], gB, ssq[:, ti:ti + 1], rstd1[:, ti:ti + 1], xn[b], ("xn", b),
                                   junk, ti % 4, hT3(ti * 128, 128), ("hT", ti), ti,
                                   copy_eng=("dve" if ti % 2 == 1 else "act")))
        if len(fins) > 3:
            fins.pop(0)()
    while fins:
        fins.pop(0)()

    if "hT" in tap_aps:
        s.add("sp", dma(tap_aps["hT"][:, :, 0:NPRE], hT_pre[:]), r=[("hT", t) for t in range(16)], dma=True)
        s.add("sp", dma(tap_aps["hT"][:, :, NPRE:NTOK], hT_own[:]), r=[("hT", t) for t in range(16, 32)], dma=True)
    if stop_after == "p1":
        return finish()

    w_in_v = w_in.rearrange("(kt p) c -> p kt c", p=128)

    s.barrier()
    ar.reset()
    gq_s = ar.alloc([128, 1], F32)
    gk_s = ar.alloc([128, 1], F32)
    m0 = ar.alloc([128, 4, 128], BF16)
    Wq3 = [ar.alloc([128, 8, 128], BF16) for _ in range(2)]
    Wk3 = [ar.alloc([128, 8, 128], BF16) for _ in range(2)]
    Wv3 = [ar.alloc([128, 8, 128], BF16) for _ in range(2)]
    Bt = [ar.alloc([128, 2, 128], BF16) for _ in range(2)]
    KT = ar.alloc([128, NTOK], BF16)
    QT = ar.alloc([128, NOWN], BF16)
    Vaug = ar.alloc([128, 32, 130], BF16)
    sq3 = [ar.alloc([128, 512], BF16) for _ in range(2)]
    rs3 = [ar.alloc([128, 512], F32) for _ in range(2)]
    kmf = ar.alloc([128, 16], F32)
    kmT = ar.alloc([128, 16], BF16)
    vm = ar.alloc([128, 16, 16], F32)
    vm2 = ar.alloc([128, 16, 16], F32)
    b31 = ar.alloc([128, H], F32)
    oneh = ar.alloc([128, 16 * 128], BF16)
    maskT0 = ar.alloc([128, NOWN], BF16)
    maskTb = ar.alloc([128, NOWN], BF16)
    gm = ar.alloc([128, 4, 16], F32)
    top8 = ar.alloc([128, 4, 8], F32)
    selt = ar.alloc([128, 4, 16], F32)
    PT = [ar.alloc([128, 512], BF16) for _ in range(3)]
    rden = ar.alloc([128, 4], F32)
    att_tm = [ar.alloc([128, 128], BF16) for _ in range(2)]

    s.add("sp", dma(gq_s, qg.rearrange("o d -> d o")), w=["gq_s"], dma=True)
    s.add("sp", dma(gk_s, kg.rearrange("o d -> d o")), w=["gk_s"], dma=True)
    s.add("pool", lambda e: e.tensor_scalar(out=gq_s, in0=gq_s, scalar1=float(DH ** -0.5), scalar2=None,
                                            op0=ALU.mult), r=["gq_s"], w=["gq_s"])
    s.add("sp", dma(vm, vm_in), w=["vm"], dma=True)
    s.add("sp", dma(vm2, vm2_in), w=["vm2"], dma=True)
    s.add("sp", dma(b31, relb[31:32, :].partition_broadcast(128)), w=["b31"], dma=True)
    s.add("pool", lambda e: e.memset(oneh, 0.0), w=["oneh"])
    s.add("pool", dma(oneh[0:16, :], oneh_in), r=["oneh"], w=["oneh"], dma=True)
    s.add("pool", lambda e: e.memset(m0, 0.0), w=["m0"])
    s.add("pool", lambda e: e.memset(maskT0, 0.0), w=["maskT0"])
    s.add("pool", lambda e: e.memset(maskTb, 0.0), w=["maskTb"])
    s.add("pool", lambda e: e.memset(Vaug[:, :, 128:129], 1.0), w=["Vaug1"])

    pt_ctr = 0
    for hi_, h in enumerate(heads):
        p = hi_ % 2
        for (W, c0_, nm) in ((Wk3, C_AK, "Wk3"), (Wq3, C_AQ, "Wq3"), (Wv3, C_AV, "Wv3")):
            s.add("pool", dma(W[p], w_in_v[:, :, c0_ + h * 128:c0_ + (h + 1) * 128]), w=[(nm, p)], dma=True)
        s.add("pool", dma(Bt[p], btab_in[h]), w=[("Bt", p)], dma=True)

        jobs = [(Wk3, "Wk3", G * 512, KT[:, G * 512:(G + 1) * 512], ("KT", G), gk_s) for G in range(8)] + \
               [(Wq3, "Wq3", NPRE + g * 512, QT[:, g * 512:(g + 1) * 512], ("QT", g), gq_s) for g in range(4)]

        def proj_mm(ji, p=p):
            W, nm, t0, dst, dkey, gcol = jobs[ji]
            ba = (2 * ji) % 6
            s.add("pe", mmgroup(PS[ba][:], [(W[p][:, kt, :], hT(kt, t0, 512)) for kt in range(8)]),
                  r=[(nm, p)] + hkeys(t0, 512), w=[("ps", ba)])

        def proj_norm(ji):
            W, nm, t0, dst, dkey, gcol = jobs[ji]
            ba, bb = (2 * ji) % 6, (2 * ji + 1) % 6
            pa, pbk = PS[ba], PS[bb]
            q = ji % 2
            s.add("act", lambda e: e.activation(out=sq3[q], in_=pa[:], func=AF.Square),
                  r=[("ps", ba)], w=[("sq3", q)])
            s.add("pe", lambda pe: pe.matmul(pbk[:], ones_bf[:], sq3[q], start=True, stop=True),
                  r=[("sq3", q), "ones_bf"], w=[("ps", bb)])
            s.add("act", lambda e: e.activation(out=rs3[q], in_=pbk[:], func=AF.Ln, scale=1.0 / DH, bias=EPS),
                  r=[("ps", bb)], w=[("rs3", q)])
            s.add("act", lambda e: e.activation(out=rs3[q], in_=rs3[q], func=AF.Exp, scale=-0.5),
                  r=[("rs3", q)], w=[("rs3", q)])
            s.add("dve", lambda e: e.scalar_tensor_tensor(out=dst, in0=pa[:], scalar=gcol, in1=rs3[q],
                                                          op0=ALU.mult, op1=ALU.mult),
                  r=[("ps", ba), ("rs3", q), "gk_s", "gq_s"], w=[dkey])
            if ji < 8:
                s.add("dve", lambda e: e.tensor_reduce(out=kmf[:, 2 * ji:2 * ji + 2],
                                                       in_=dst.rearrange("p (n s) -> p n s", s=256),
                                                       axis=AX.X, op=ALU.add),
                      r=[dkey], w=[("kmf", ji)])

        proj_mm(0)
        for ji in range(len(jobs)):
            if ji + 1 < len(jobs):
                proj_mm(ji + 1)
            proj_norm(ji)
        if stop_after == "p3a":
            break
        s.add("dve", lambda e: e.tensor_scalar(out=kmT, in0=kmf, scalar1=1.0 / 256, scalar2=None,
                                               op0=ALU.mult), r=[("kmf", G) for G in range(8)], w=["kmT"])
        def vproj(G, p=p):
            bk = (2 * G) % 6

            def vmm(pe, G=G, bk=bk, p=p):
                last = None
                for i in range(4):
                    t0 = (4 * G + i) * 128
                    for kt in range(8):
                        last = pe.matmul(PS[bk][:, i * 128:(i + 1) * 128], hT(kt, t0, 128), Wv3[p][:, kt, :],
                                         start=(kt == 0), stop=(kt == 7))
                return last
            s.add("pe", vmm, r=[("Wv3", p)] + hkeys(G * 512, 512), w=[("ps", bk)])
            s.add("act", lambda e, G=G, bk=bk: e.activation(out=Vaug[:, 4 * G:4 * G + 4, 0:128],
                                                            in_=PS[bk][:].rearrange("p (i d) -> p i d", i=4),
                                                            func=AF.Copy),
                  r=[("ps", bk)], w=[("V", G)])

        for g in range(4):
            def gate_mm(pe, g=g):
                last = None
                for i in range(4):
                    last = pe.matmul(PS[6][:, i * 16:(i + 1) * 16], QT[:, g * 512 + i * 128:g * 512 + (i + 1) * 128],
                                     kmT, start=True, stop=True)
                return last
            DBG = 99
            if DBG < 1:
                break
            s.add("pe", gate_mm, r=[("QT", g), "kmT"], w=["ps6a"])
            if DBG < 2:
                break
            s.add("dve", lambda e, g=g: e.tensor_tensor(out=gm, in0=PS[6][:, 0:64].rearrange("p (i n) -> p i n", i=4),
                                                        in1=vm[:, 4 * g:4 * g + 4, :], op=ALU.add),
                  r=["ps6a", "vm"], w=["gm"])
            if DBG < 3:
                break
            for i in range(4):
                s.add("dve", lambda e, i=i: e.max(out=top8[:, i, :], in_=gm[:, i, :]), r=["gm"], w=[("top8", i)])
            if DBG < 4:
                break
            for i in range(4):
                s.add("dve", lambda e, i=i: e.tensor_scalar(out=selt[:, i, :], in0=gm[:, i, :],
                                                            scalar1=top8[:, i, 2:3], scalar2=30000.0,
                                                            op0=ALU.is_ge, op1=ALU.mult),
                      r=["gm", ("top8", i)], w=[("selt", i)])
            s.add("dve", lambda e, g=g: e.tensor_tensor(out=m0[:, :, 0:16], in0=selt, in1=vm2[:, 4 * g:4 * g + 4, :],
                                                        op=ALU.add),
                  r=[("selt", i) for i in range(4)] + ["vm2"], w=["m0"])

            if DBG < 5:
                break

            def mtr(pe):
                last = None
                for i in range(4):
                    last = pe.matmul(PS[5][:, i * 128:(i + 1) * 128], m0[:, i, :], ident[:], start=True, stop=True)
                return last
            vproj(2 * g)
            vproj(2 * g + 1)
            s.add("pe", mtr, r=["m0", "ident"], w=[("ps", 5)])
            if DBG < 6:
                break
            s.add("act", lambda e, g=g: e.activation(out=maskT0[:, g * 512:(g + 1) * 512],
                                                     in_=PS[5][:, :],
                                                     func=AF.Copy), r=[("ps", 5), "maskT0"], w=[("maskT0", g)])
            if DBG < 7:
                break
            s.add("act", lambda e, g=g, h=h: e.activation(out=maskTb[:, g * 512:(g + 1) * 512],
                                                          in_=PS[5][:, :], func=AF.Identity, bias=b31[:, h:h + 1]),
                  r=[("ps", 5), "b31", "maskTb"], w=[("maskTb", g)])
            if DBG < 8:
                break

        if stop_after == "p3b":
            break
        for g in range(4):
            c0 = 8 + 2 * g
            nkt = 2 * c0 + 4
            qc0 = g * 512

            def classify(n, r_, i, c0=c0):
                qb_, qi = c0 + i // 2, i % 2
                if n > qb_:
                    return "SKIP"
                if n == qb_:
                    if r_ == qi:
                        return "DIAG"
                    return "OFF" if r_ < qi else "SKIP"
                if n == qb_ - 1 and r_ == 1 and qi == 0:
                    return "NEAR"
                return "FAR"

            def qk_op(j, g=g, qc0=qc0, p=p):
                n, r_ = j // 2, j % 2
                cls = [classify(n, r_, i) for i in range(4)]
                lo = min(i for i in range(4) if cls[i] != "SKIP")
                assert all(c != "SKIP" for c in cls[lo:])
                bank = j % 2
                st = PS[bank]
                terms = [(lo * 128, 512, KT[:, j * 128:(j + 1) * 128], QT[:, qc0 + lo * 128:qc0 + 512])]
                i = lo
                while i < 4:
                    if cls[i] == "FAR":
                        k = i
                        while k < 4 and cls[k] == "FAR":
                            k += 1
                        terms.append((i * 128, k * 128, oneh[:, n * 128:(n + 1) * 128],
                                      maskTb[:, qc0 + i * 128:qc0 + k * 128]))
                        i = k
                        continue
                    if cls[i] == "NEAR":
                        terms.append((i * 128, (i + 1) * 128, oneh[:, n * 128:(n + 1) * 128],
                                      maskT0[:, qc0 + i * 128:qc0 + (i + 1) * 128]))
                        terms.append((i * 128, (i + 1) * 128, ident[:], Bt[p][:, 1, :]))
                    elif cls[i] == "DIAG":
                        terms.append((i * 128, (i + 1) * 128, ident[:], Bt[p][:, 0, :]))
                    elif cls[i] == "OFF":
                        terms.append((i * 128, (i + 1) * 128, ident[:], Bt[p][:, 1, :]))
                    i += 1

                def fn(pe):
                    last = None
                    for ti_, (a, b_, l, r2) in enumerate(terms):
                        last = pe.matmul(st[:, a:b_], l, r2, start=(ti_ == 0), stop=(ti_ == len(terms) - 1))
                    return last
                s.add("pe", fn, r=[("KT", j // 4), ("QT", g), ("maskT0", g), ("maskTb", g), ("Bt", p), "oneh", "ident"],
                      w=[("ps", bank)])
                return lo, bank

            def exp_op(j, lo, bank, pt):
                s.add("act", lambda e: e.activation(out=PT[pt][:, lo * 128:512], in_=PS[bank][:, lo * 128:512],
                                                    func=AF.Exp), r=[("ps", bank)], w=[("PT", pt)])

            def pv_op(j, lo, pt, c0=c0):
                last_js = [2 * (c0 + i // 2) + (i % 2) for i in range(4)]

                def fn(pe):
                    last = None
                    for i in range(lo, 4):
                        last = pe.matmul(PS[2 + i][:, 0:129], PT[pt][:, i * 128:(i + 1) * 128], Vaug[:, j, 0:129],
                                         start=(j == 0), stop=(j == last_js[i]))
                    return last
                s.add("pe", fn, r=[("PT", pt), ("V", j // 4), "Vaug1"], w=[("ps", 2 + i) for i in range(lo, 4)])

            DA = 99
            prev = None
            for j in range(nkt):
                lo, bank = qk_op(j)
                pt = pt_ctr % 3
                pt_ctr += 1
                exp_op(j, lo, bank, pt)
                if prev is not None and DA >= 2:
                    pv_op(*prev)
                prev = (j, lo, pt)
            if DA < 2:
                break
            pv_op(*prev)
            if DA < 3:
                break
            for i in range(4):
                a = i % 2
                s.add("dve", lambda e, i=i: e.reciprocal(out=rden[:, i:i + 1], in_=PS[2 + i][:, 128:129]),
                      r=[("ps", 2 + i)], w=[("rden", i)])
                if DA < 4:
                    continue
                s.add("act", lambda e, i=i, a=a: e.activation(out=att_tm[a], in_=PS[2 + i][:, 0:128],
                                                              func=AF.Copy, scale=rden[:, i:i + 1]),
                      r=[("ps", 2 + i), ("rden", i)], w=[("att_tm", a)])
                if DA < 5:
                    continue
                s.add("pe", lambda pe, i=i, a=a: pe.transpose(out=PS[6][:].bitcast(BF16)[:, 512 + i * 128:512 + (i + 1) * 128],
                                                              in_=att_tm[a], identity=ident[:]),
                      r=[("att_tm", a), "ident"], w=[("ps6b", i)])
            if DA < 6:
                break
            s.add("act", lambda e, g=g, h=h: e.activation(out=attT[:, h, g * 512:(g + 1) * 512],
                                                          in_=PS[6][:].bitcast(BF16)[:, 512:1024], func=AF.Copy),
                  r=[("ps6b", i) for i in range(4)], w=[("attT", h, g)])
            if DA < 7:
                break

    for nm_, t_ in (("KT", KT), ("QT", QT), ("Vaug", Vaug), ("maskT0", maskT0), ("maskTb", maskTb), ("attT", attT[:])):
        if nm_ in tap_aps:
            keys = {"KT": [("KT", G) for G in range(8)], "QT": [("QT", g) for g in range(4)],
                    "Vaug": [("V", G) for G in range(8)] + ["Vaug1"],
                    "maskT0": [("maskT0", g) for g in range(4)], "maskTb": [("maskTb", g) for g in range(4)],
                    "attT": [("attT", hh, g) for hh in heads for g in range(4)]}[nm_]
            if nm_ == "attT":
                for hh in heads:
                    for g_ in range(4):
                        s.add("sp", dma(tap_aps[nm_][:, hh, g_ * 512:(g_ + 1) * 512],
                                        attT[:, hh, g_ * 512:(g_ + 1) * 512]), r=[("attT", hh, g_)], dma=True)
            else:
                s.add("sp", dma(tap_aps[nm_], t_), r=keys, dma=True)
    if stop_after in ("p3", "p3a", "p3b"):
        return finish()

    s.barrier()
    ar.reset()
    lbraw = ar.alloc([128, 2, H], F32)
    lb_s = ar.alloc([128, H], F32)
    lnoml = ar.alloc([128, H], F32)
    tmp8 = ar.alloc([128, H], F32)
    hgain = ar.alloc([128, 1], F32)
    resetm = ar.alloc([128, 512], F32)
    tri3 = ar.alloc([128, 4, 128], BF16)
    lbrawT = ar.alloc([128, 128], F32)
    Wf2 = [ar.alloc([128, 8, 128], BF16) for _ in range(2)]
    Wi2 = [ar.alloc([128, 8, 128], BF16) for _ in range(2)]
    Wq2 = [ar.alloc([128, 8, 128], BF16) for _ in range(2)]
    Wg2 = [ar.alloc([128, 8, 128], BF16) for _ in range(2)]
    S_f = [ar.alloc([128, 128], F32) for _ in range(2)]
    S_bf = [ar.alloc([128, 128], BF16) for _ in range(2)]

    def two(shape, dt):
        return [ar.alloc(shape, dt) for _ in range(2)]
    eu_t = two([128, 512], F32)
    L1_t = two([128, 512], F32)
    L2_t = two([128, 512], F32)
    bc_t = two([128, 512], F32)
    rs_t = two([128, 512], F32)
    sg_t = two([128, 512], F32)
    m1_t = two([128, 512], F32)
    Eq_t = two([128, 512], BF16)
    Eb_t = two([128, 512], BF16)
    Eqo_t = two([128, 512], BF16)
    tmpA1 = ar.alloc([128, 512], BF16)
    tmpA = [tmpA1, tmpA1]
    kdT = two([128, 512], BF16)
    ke_t = two([128, 512], BF16)
    qe_t = two([128, 512], BF16)
    qb_t = two([128, 512], BF16)
    keo_t = two([128, 512], BF16)
    qeo_t = two([128, 512], BF16)
    osq1 = ar.alloc([128, 512], BF16)
    osq = [osq1, osq1]
    zq_sb = two([128, 512], BF16)
    v_tm = two([128, 4, 128], BF16)
    kd_tm = two([128, 4, 128], BF16)
    A_T = two([128, 4, 128], BF16)
    biasK = two([128, 8], F32)
    nbmid = two([128, 8], F32)
    biasD = two([128, 4], F32)
    biasKo = two([128, 4], F32)
    nb63 = two([128, 4], F32)
    ebl = two([128, 4], F32)

    s.add("pool", lambda e: e.memset(lbrawT, 0.0), w=["lbrawT"])
    s.add("sp", dma(lbrawT[0:16, :], lbg.rearrange("l (h p) -> (l h) p", p=128)), r=["lbrawT"], w=["lbrawT"], dma=True)
    s.add("pe", lambda pe: pe.transpose(out=PS[0][:, 0:128], in_=lbrawT, identity=identf[:]),
          r=["lbrawT", "identf"], w=[("ps", 0)])
    s.add("act", lambda e: e.activation(out=lbraw.rearrange("p l h -> p (l h)"), in_=PS[0][:, 0:16], func=AF.Copy),
          r=[("ps", 0)], w=["lbraw"])
    s.add("sp", dma(hgain, hgg.rearrange("o d -> d o")), w=["hgain"], dma=True)
    s.add("dve", lambda e: e.tensor_tensor(out=tmp8, in0=lbraw[:, 1, :], in1=lbraw[:, 0, :], op=ALU.subtract),
          r=["lbraw"], w=["tmp8"])
    s.add("act", lambda e: e.activation(out=tmp8, in_=tmp8, func=AF.Exp), r=["tmp8"], w=["tmp8"])
    s.add("dve", lambda e: e.tensor_scalar(out=tmp8, in0=tmp8, scalar1=1.0, scalar2=None, op0=ALU.add),
          r=["tmp8"], w=["tmp8"])
    s.add("dve", lambda e: e.reciprocal(out=lb_s, in_=tmp8), r=["tmp8"], w=["lb_s"])
    s.add("dve", lambda e: e.tensor_scalar(out=tmp8, in0=lb_s, scalar1=-1.0, scalar2=1.0, op0=ALU.mult, op1=ALU.add),
          r=["lb_s", "tmp8"], w=["tmp8"])
    s.add("act", lambda e: e.activation(out=lnoml, in_=tmp8, func=AF.Ln), r=["tmp8"], w=["lnoml"])
    s.add("pool", lambda e: e.memset(resetm, 1.0), w=["resetm"])
    s.add("pool", lambda e: e.memset(resetm.rearrange("p (c t) -> p c t", t=128)[:, :, 0:1], 0.0),
          r=["resetm"], w=["resetm"])
    s.add("pool", lambda e: e.memset(tri3, 1.0), w=["tri3"])
    s.add("pool", lambda e: e.affine_select(out=tri3, in_=tri3, pattern=[[0, 4], [1, 128]],
                                            compare_op=ALU.is_ge, fill=0.0, base=0, channel_multiplier=-1),
          r=["tri3"], w=["tri3"])
    s.add("pool", lambda e: e.affine_select(out=tri3[:, :, 64:128], in_=tri3[:, :, 64:128], pattern=[[0, 4], [1, 64]],
                                            compare_op=ALU.is_ge, fill=0.0, base=-8192, channel_multiplier=128),
          r=["tri3"], w=["tri3"])
    for q in range(2):
        s.add("pool", lambda e, q=q: e.memset(keo_t[q], 0.0), w=[("keo", q)])
        s.add("pool", lambda e, q=q: e.memset(Eqo_t[q], 0.0), w=[("Eqo", q)])

    QS = float(DH ** -0.5)
    LNQS = float(math.log(QS))
    state = {"sfi": 0, "sbi": 0}

    SEL = [set()]

    def sa(stage, *a_, **k_):
        if stage in SEL[0]:
            s.add(*a_, **k_)

    def frontA(h, p, G):
        q = G % 2
        own = G >= 4
        t0 = G * 512
        hk = hkeys(t0, 512)
        eu, L1, L2, bc = eu_t[q], L1_t[q], L2_t[q], bc_t[q]
        bz, bq = G % 2, 1 - G % 2
        sa("Pz", "pe", mmgroup(PS[bz][:], [(Wf2[p][:, kt, :], hT(kt, t0, 512)) for kt in range(8)]),
              r=[("Wf2", p)] + hk, w=[("ps", bz)])
        def vmm2(pe):
            last = None
            for i in range(4):
                tt0 = (4 * G + i) * 128
                for kt in range(8):
                    last = pe.matmul(PS[2][:, i * 128:(i + 1) * 128], hT(kt, tt0, 128), Wi2[p][:, kt, :],
                                     start=(kt == 0), stop=(kt == 7))
            return last
        sa("Pv", "pe", vmm2, r=[("Wi2", p)] + hk, w=[("ps", 2)])
        sa("Pv", "act", lambda e: e.activation(out=v_tm[q], in_=PS[2][:].rearrange("p (i d) -> p i d", i=4),
                                            func=AF.Copy), r=[("ps", 2)], w=[("v_tm", q)])
        if own:
            sa("Pq", "pe", mmgroup(PS[bq][:], [(Wq2[p][:, kt, :], hT(kt, t0, 512)) for kt in range(8)]),
                  r=[("Wq2", p)] + hk, w=[("ps", bq)])
            sa("Pq", "dve", lambda e: e.tensor_copy(out=zq_sb[q], in_=PS[bq][:]), r=[("ps", bq)], w=[("zq_sb", q)])

    def frontB(h, p, G):
        q = G % 2
        own = G >= 4
        t0 = G * 512
        hk = hkeys(t0, 512)
        eu, L1, L2, bc = eu_t[q], L1_t[q], L2_t[q], bc_t[q]
        bz = G % 2
        sa("A1", "act", lambda e: e.activation(out=eu, in_=PS[bz][:], func=AF.Exp, scale=-1.0),
              r=[("ps", bz)], w=[("eu", q)])
        sa("A1", "act", lambda e: e.activation(out=L1, in_=eu, func=AF.Ln, bias=1.0), r=[("eu", q)], w=[("L1", q)])
        sa("A1", "act", lambda e: e.activation(out=L2, in_=eu, func=AF.Ln, bias=1.0, scale=lb_s[:, h:h + 1]),
              r=[("eu", q), "lb_s"], w=[("L2", q)])
        sa("D2", "dve", lambda e: e.tensor_tensor(out=L2, in0=L2, in1=L1, op=ALU.subtract),
              r=[("L1", q), ("L2", q)], w=[("L2", q)])
        sa("D2", "dve", lambda e: e.tensor_tensor_scan(out=bc, data0=resetm, data1=L2, initial=0.0,
                                                    op0=ALU.mult, op1=ALU.add),
              r=[("L2", q), "resetm"], w=[("bc", q)])
        sa("D2", "dve", lambda e: e.tensor_tensor(out=eu, in0=PS[bz][:], in1=L1, op=ALU.add),
              r=[("ps", bz), ("L1", q), ("eu", q)], w=[("eu", q)])
        sa("D2", "dve", lambda e: e.scalar_tensor_tensor(out=eu, in0=eu, scalar=-1.0, in1=bc,
                                                      op0=ALU.mult, op1=ALU.subtract),
              r=[("eu", q), ("bc", q)], w=[("eu", q)])
        bc3 = bc.rearrange("p (i t) -> p i t", t=128)
        bc4 = bc.rearrange("p (c t) -> p c t", t=64)
        sa("D2", "dve", lambda e: e.tensor_scalar(out=biasD[q], in0=bc3[:, :, 127], scalar1=lnoml[:, h:h + 1],
                                               scalar2=None, op0=ALU.add),
              r=[("bc", q), "lnoml"], w=[("biasD", q)])
        sa("A3", "act", lambda e: e.activation(out=ebl[q], in_=bc3[:, :, 127], func=AF.Exp),
              r=[("bc", q)], w=[("ebl", q)])
        for i in range(4):
            sa("A3", "act", lambda e, i=i: e.activation(out=kdT[q][:, i * 128:(i + 1) * 128],
                                                        in_=eu[:, i * 128:(i + 1) * 128], func=AF.Exp,
                                                        bias=biasD[q][:, i:i + 1]),
               r=[("eu", q), ("biasD", q)], w=[("kdT", q, i)])
        if own:
            sa("D2", "dve", lambda e: e.tensor_scalar(out=biasK[q], in0=bc4[:, :, 31], scalar1=lnoml[:, h:h + 1],
                                                   scalar2=None, op0=ALU.add),
                  r=[("bc", q), "lnoml"], w=[("biasK", q)])
            sa("D2", "dve", lambda e: e.tensor_scalar(out=nbmid[q], in0=bc4[:, :, 31], scalar1=-1.0,
                                                      scalar2=LNQS, op0=ALU.mult, op1=ALU.add),
               r=[("bc", q)], w=[("nbmid", q)])
            sa("D2", "dve", lambda e: e.tensor_scalar(out=biasKo[q], in0=bc3[:, :, 63], scalar1=lnoml[:, h:h + 1],
                                                      scalar2=None, op0=ALU.add),
               r=[("bc", q), "lnoml"], w=[("biasKo", q)])
            sa("D2", "dve", lambda e: e.tensor_scalar(out=nb63[q], in0=bc3[:, :, 63], scalar1=-1.0,
                                                      scalar2=LNQS, op0=ALU.mult, op1=ALU.add),
               r=[("bc", q)], w=[("nb63", q)])
            eu3 = eu.rearrange("p (i t) -> p i t", t=128)
            eu4 = eu.rearrange("p (c t) -> p c t", t=64)
            sa("A3", "dve", lambda e: e.tensor_tensor(out=L2.rearrange("p (c t) -> p c t", t=64), in0=eu4,
                                                      in1=biasK[q].unsqueeze(2).broadcast_to([128, 8, 64]), op=ALU.add),
               r=[("eu", q), ("biasK", q), ("L2", q)], w=[("L2", q)])
            sa("A3", "act", lambda e: e.activation(out=ke_t[q], in_=L2, func=AF.Exp),
               r=[("L2", q)], w=[("ke", q, c) for c in range(8)])
            sa("A3", "dve", lambda e: e.tensor_tensor(out=L1.rearrange("p (c t) -> p c t", t=64), in0=bc4,
                                                      in1=nbmid[q].unsqueeze(2).broadcast_to([128, 8, 64]), op=ALU.add),
               r=[("bc", q), ("nbmid", q), ("L1", q)], w=[("L1", q)])
            sa("A3", "act", lambda e: e.activation(out=Eq_t[q], in_=L1, func=AF.Exp),
               r=[("L1", q)], w=[("Eq", q, c) for c in range(8)])
            sa("A3", "dve", lambda e: e.tensor_tensor(out=L2.rearrange("p (i t) -> p i t", t=128)[:, :, 0:64],
                                                      in0=eu3[:, :, 0:64],
                                                      in1=biasKo[q].unsqueeze(2).broadcast_to([128, 4, 64]), op=ALU.add),
               r=[("eu", q), ("biasKo", q), ("L2", q)], w=[("L2", q)])
            sa("A3", "act", lambda e: e.activation(out=keo_t[q].rearrange("p (i t) -> p i t", t=128)[:, :, 0:64],
                                                   in_=L2.rearrange("p (i t) -> p i t", t=128)[:, :, 0:64], func=AF.Exp),
               r=[("L2", q), ("keo", q)], w=[("keo", q, i) for i in range(4)])
            sa("A3", "dve", lambda e: e.tensor_tensor(out=L1.rearrange("p (i t) -> p i t", t=128)[:, :, 64:128],
                                                      in0=bc3[:, :, 64:128],
                                                      in1=nb63[q].unsqueeze(2).broadcast_to([128, 4, 64]), op=ALU.add),
               r=[("bc", q), ("nb63", q), ("L1", q)], w=[("L1", q)])
            sa("A3", "act", lambda e: e.activation(out=Eqo_t[q].rearrange("p (i t) -> p i t", t=128)[:, :, 64:128],
                                                   in_=L1.rearrange("p (i t) -> p i t", t=128)[:, :, 64:128], func=AF.Exp),
               r=[("L1", q), ("Eqo", q)], w=[("Eqo", q, i) for i in range(4)])
            sa("A3", "act", lambda e: e.activation(out=Eb_t[q], in_=bc, func=AF.Exp, bias=LNQS), r=[("bc", q)], w=[("Eb", q)])
            sa("D4", "pool", lambda e: e.tensor_tensor(out=qe_t[q], in0=zq_sb[q], in1=Eq_t[q], op=ALU.mult),
               r=[("zq_sb", q)] + [("Eq", q, c) for c in range(8)], w=[("qe", q)])
            sa("D4", "pool", lambda e: e.tensor_tensor(out=qeo_t[q], in0=zq_sb[q], in1=Eqo_t[q], op=ALU.mult),
               r=[("zq_sb", q), ("Eqo", q)] + [("Eqo", q, i) for i in range(4)], w=[("qeo", q)])
            sa("D4", "pool", lambda e: e.tensor_tensor(out=qb_t[q], in0=zq_sb[q], in1=Eb_t[q], op=ALU.mult),
               r=[("zq_sb", q), ("Eb", q)], w=[("qb", q)])

    def back1(h, p, G):
        q = G % 2
        own = G >= 4
        t0 = G * 512
        hk = hkeys(t0, 512)
        def kdtr(pe):
            pv = PS[3][:].bitcast(BF16)
            last = None
            for i in range(4):
                last = pe.transpose(out=pv[:, i * 128:(i + 1) * 128], in_=kdT[q][:, i * 128:(i + 1) * 128],
                                    identity=ident[:])
            return last
        sa("R1", "pe", kdtr, r=[("kdT", q, i) for i in range(4)] + ["ident"], w=[("ps", 3)])
        if own:
            sa("R1", "dve", lambda e: e.tensor_copy(out=kd_tm[q], in_=PS[3][:].bitcast(BF16)[:, 0:512].rearrange(
                "p (c k) -> p c k", c=4)), r=[("ps", 3)], w=[("kd_tm", q)])
        else:
            sa("R1", "act", lambda e: e.activation(out=kd_tm[q], in_=PS[3][:].bitcast(BF16)[:, 0:512].rearrange(
                "p (c k) -> p c k", c=4), func=AF.Copy), r=[("ps", 3)], w=[("kd_tm", q)])

        if own:
            def amm(pe):
                last = None
                for i in range(4):
                    last = pe.matmul(PS[4][:, i * 128:(i + 1) * 128], ke_t[q][:, i * 128:(i + 1) * 128],
                                     qe_t[q][:, i * 128:(i + 1) * 128], start=True, stop=True)
                for i in range(4):
                    last = pe.matmul(PS[7][:, i * 128:(i + 1) * 128], keo_t[q][:, i * 128:(i + 1) * 128],
                                     qeo_t[q][:, i * 128:(i + 1) * 128], start=True, stop=True)
                return last
            sa("R1", "pe", amm, r=[("ke", q, c) for c in range(8)] + [("keo", q, i) for i in range(4)] +
                  [("qe", q), ("qeo", q), ("keo", q)], w=[("ps", 4), ("ps", 7)])
            sa("R2", "dve", lambda e: e.tensor_tensor(out=tmpA[q].rearrange("p (i t) -> p i t", i=4),
                                                   in0=PS[4][:].rearrange("p (i t) -> p i t", i=4),
                                                   in1=tri3, op=ALU.mult),
                  r=[("ps", 4), "tri3"], w=["tmpA"])
            sa("R2", "dve", lambda e: e.tensor_tensor(out=A_T[q].rearrange("p i t -> p (i t)"), in0=tmpA[q], in1=PS[7][:],
                                                   op=ALU.add),
                  r=["tmpA", ("ps", 7)], w=[("A_T", q)])

        if "CH" not in SEL[0]:
            return
        if G == 0:
            sfi0 = state["sfi"]
            sa("CH", "pool", lambda e: e.memset(S_f[sfi0], 0.0), w=[("S_f", sfi0)])

        def snew(pe):
            last = None
            for i in range(4):
                last = pe.matmul(PS[6][:, i * 128:(i + 1) * 128], kd_tm[q][:, i, :], v_tm[q][:, i, :],
                                 start=True, stop=True)
            return last
        sa("CH", "pe", snew, r=[("kd_tm", q), ("v_tm", q)], w=[("ps", 6)])
        for i in range(4):
            sfi, sbi = state["sfi"], state["sbi"]
            if own:
                def omm(pe, i=i, sbi=sbi):
                    pe.matmul(PS[5][:, i * 128:(i + 1) * 128], S_bf[sbi], qb_t[q][:, i * 128:(i + 1) * 128],
                              start=True, stop=False)
                    return pe.matmul(PS[5][:, i * 128:(i + 1) * 128], v_tm[q][:, i, :], A_T[q][:, i, :],
                                     start=False, stop=True)
                sa("CH", "pe", omm, r=[("S_bf", sbi), ("qb", q), ("v_tm", q), ("A_T", q)], w=[("ps", 5)])
            nsf = 1 - sfi
            sa("CH", "dve", lambda e, i=i, sfi=sfi, nsf=nsf: e.scalar_tensor_tensor(
                out=S_f[nsf], in0=S_f[sfi], scalar=ebl[q][:, i:i + 1],
                in1=PS[6][:, i * 128:(i + 1) * 128], op0=ALU.mult, op1=ALU.add),
                r=[("S_f", sfi), ("ebl", q), ("ps", 6)], w=[("S_f", nsf)])
            state["sfi"] = nsf
            need_cast = (G == 3 and i == 3) or (G >= 4 and not (G == 7 and i == 3))
            if need_cast:
                nsb = 1 - sbi
                state["sbi"] = nsb
                sa("CH", "act", lambda e, nsb=nsb, nsf=nsf: e.activation(out=S_bf[nsb], in_=S_f[nsf], func=AF.Copy),
                      r=[("S_f", nsf)], w=[("S_bf", nsb)])

    def back2(h, p, G):
        q = G % 2
        own = G >= 4
        t0 = G * 512
        hk = hkeys(t0, 512)
        if own:
            g = G - 4
            rs, sg, m1 = rs_t[q], sg_t[q], m1_t[q]
            sa("Na", "act", lambda e: e.activation(out=osq[q], in_=PS[5][:], func=AF.Square), r=[("ps", 5)], w=["osq"])
            sa("Na", "pe", lambda pe: pe.matmul(PS[4][:], ones_bf[:], osq[q], start=True, stop=True),
                  r=["osq", "ones_bf"], w=[("ps", 4)])
            sa("Na", "act", lambda e: e.activation(out=rs, in_=PS[4][:], func=AF.Ln, scale=1.0 / DH, bias=EPS),
                  r=[("ps", 4)], w=[("rs", q)])
            sa("Na", "act", lambda e: e.activation(out=rs, in_=rs, func=AF.Exp, scale=-0.5), r=[("rs", q)], w=[("rs", q)])
            sa("Na", "dve", lambda e: e.scalar_tensor_tensor(out=m1, in0=PS[5][:], scalar=hgain, in1=rs,
                                                          op0=ALU.mult, op1=ALU.mult),
                  r=[("ps", 5), ("rs", q), "hgain"], w=[("m1", q)])
            sa("ZG", "pe", mmgroup(PS[7][:], [(Wg2[p][:, kt, :], hT(kt, t0, 512)) for kt in range(8)]),
                  r=[("Wg2", p)] + hk, w=[("ps", 7)])
            sa("Na", "act", lambda e: e.activation(out=sg, in_=PS[7][:], func=AF.Exp, scale=-1.0),
                  r=[("ps", 7)], w=[("sg", q)])
            sa("Na", "act", lambda e: e.activation(out=sg, in_=sg, func=AF.Ln, bias=1.0), r=[("sg", q)], w=[("sg", q)])
            sa("Na", "act", lambda e: e.activation(out=sg, in_=sg, func=AF.Exp, scale=-1.0), r=[("sg", q)], w=[("sg", q)])
            sa("Nb", "dve", lambda e: e.tensor_tensor(out=sg, in0=PS[7][:], in1=sg, op=ALU.mult),
                  r=[("ps", 7), ("sg", q)], w=[("sg", q)])
            sa("Nb", "pool", lambda e: e.tensor_tensor(out=ohgT[:, h, g * 512:(g + 1) * 512], in0=m1, in1=sg, op=ALU.mult),
                  r=[("m1", q), ("sg", q)], w=[("ohgT", h, g)])

    items = [(hi_, h, G) for hi_, h in enumerate(heads) for G in range(8)]

    def wload(hi_):
        h = heads[hi_]
        p = hi_ % 2
        for (W, c0_, nm) in ((Wf2, C_HF, "Wf2"), (Wi2, C_HI, "Wi2"), (Wq2, C_HQ, "Wq2"), (Wg2, C_HG, "Wg2")):
            s.add("pool", dma(W[p], w_in_v[:, :, c0_ + h * 128:c0_ + (h + 1) * 128]), w=[(nm, p)], dma=True)

    def run(fn, item, *stages):
        if item is None:
            return
        hi_, h, G = item
        SEL[0] = set(stages)
        fn(h, hi_ % 2, G)

    wload(0)
    if len(heads) > 1:
        wload(1)
    run(frontA, items[0], "Pz", "Pv", "Pq")
    for st in ("A1", "D2", "A3", "D4"):
        run(frontB, items[0], st)
    run(frontA, items[1], "Pz")
    for k, it in enumerate(items):
        nx = items[k + 1] if k + 1 < len(items) else None
        nx2 = items[k + 2] if k + 2 < len(items) else None
        if it[2] == 1 and it[0] >= 1 and it[0] + 1 < len(heads):
            wload(it[0] + 1)
        run(frontB, nx, "A1")
        run(back1, it, "R1")
        run(back1, it, "R2")
        run(frontA, nx, "Pv")
        run(frontB, nx, "D2")
        run(back2, it, "ZG")
        run(back1, it, "CH")
        run(frontB, nx, "A3")
        run(frontA, nx, "Pq")
        run(back2, it, "Na")
        run(frontB, nx, "D4")
        run(back2, it, "Nb")
        run(frontA, nx2, "Pz")

    if "ohgT" in tap_aps:
        for hh in heads:
            s.add("sp", dma(tap_aps["ohgT"][:, hh, :], ohgT[:, hh, :]), r=[("ohgT", hh, g) for g in range(4)], dma=True)
    if stop_after == "p2":
        return finish()

    s.barrier()
    ar.reset()
    mergedT = ar.alloc([128, 8, NOWN], BF16)
    off_after_merged = ar.off
    ar2 = Arena(hT_pre[:].rearrange("p k t -> p (k t)").bitcast(F32), 8 * NPRE // 2)
    Wo = ar2.alloc([128, 8, D], BF16)
    W4 = {nm: [ar.alloc([128, 8, 128], BF16) for _ in range(2)] for nm in ("bh", "ba", "g0", "g1")}
    th0 = [ar.alloc([128, 512], F32) for _ in range(2)]
    th1 = [ar.alloc([128, 512], F32) for _ in range(2)]
    ma_t = [ar.alloc([128, 512], F32) for _ in range(2)]
    mb_t = [ar.alloc([128, 512], F32) for _ in range(2)]
    w_bh_v = w_bh.rearrange("(kt p) c -> p kt c", p=128)
    w_ba_v = w_ba.rearrange("(kt p) c -> p kt c", p=128)
    it = 0

    def w4load(ct):
        p = ct % 2
        s.add("pool", dma(W4["bh"][p], w_bh_v[:, :, ct * 128:(ct + 1) * 128]), w=[("W4bh", p)], dma=True)
        s.add("pool", dma(W4["ba"][p], w_ba_v[:, :, ct * 128:(ct + 1) * 128]), w=[("W4ba", p)], dma=True)
        s.add("pool", dma(W4["g0"][p], w_in_v[:, :, C_GL + ct * 128:C_GL + (ct + 1) * 128]), w=[("W4g0", p)], dma=True)
        s.add("pool", dma(W4["g1"][p], w_in_v[:, :, C_GL + D + ct * 128:C_GL + D + (ct + 1) * 128]),
              w=[("W4g1", p)], dma=True)
    w4load(0)
    w4load(1)
    for ct in range(8):
        p = ct % 2
        for g in range(4):
            q = it % 2
            it += 1
            b0 = 4 * q
            cols = slice(g * 512, (g + 1) * 512)
            s.add("pe", mmgroup(PS[b0][:], [(W4["bh"][p][:, kt, :], ohgT[:, kt, cols]) for kt in range(8)]),
                  r=[("W4bh", p)] + [("ohgT", kt, g) for kt in range(8)], w=[("ps", b0)])
            s.add("pe", mmgroup(PS[b0 + 1][:], [(W4["ba"][p][:, kt, :], attT[:, kt, cols]) for kt in range(8)]),
                  r=[("W4ba", p)] + [("attT", kt, g) for kt in range(8)], w=[("ps", b0 + 1)])
            s.add("pe", mmgroup(PS[b0 + 2][:], [(W4["g0"][p][:, kt, :], hT_own[:, kt, cols]) for kt in range(8)]),
                  r=[("W4g0", p)] + hkeys(NPRE + g * 512, 512), w=[("ps", b0 + 2)])
            s.add("pe", mmgroup(PS[b0 + 3][:], [(W4["g1"][p][:, kt, :], hT_own[:, kt, cols]) for kt in range(8)]),
                  r=[("W4g1", p)] + hkeys(NPRE + g * 512, 512), w=[("ps", b0 + 3)])
            s.add("act", lambda e, q=q, b0=b0: e.activation(out=th0[q], in_=PS[b0 + 2][:], func=AF.Tanh, scale=0.5),
                  r=[("ps", b0 + 2)], w=[("th0", q)])
            s.add("act", lambda e, q=q, b0=b0: e.activation(out=th1[q], in_=PS[b0 + 3][:], func=AF.Tanh, scale=0.5),
                  r=[("ps", b0 + 3)], w=[("th1", q)])
            s.add("dve", lambda e, q=q, b0=b0: e.scalar_tensor_tensor(out=ma_t[q], in0=th0[q], scalar=1.0,
                                                                      in1=PS[b0][:], op0=ALU.add, op1=ALU.mult),
                  r=[("th0", q), ("ps", b0)], w=[("ma", q)])
            s.add("dve", lambda e, q=q, b0=b0: e.scalar_tensor_tensor(out=mb_t[q], in0=th1[q], scalar=1.0,
                                                                      in1=PS[b0 + 1][:], op0=ALU.add, op1=ALU.mult),
                  r=[("th1", q), ("ps", b0 + 1)], w=[("mb", q)])
            s.add("dve", lambda e, q=q, ct=ct, cols=cols: e.tensor_tensor(out=mergedT[:, ct, cols], in0=ma_t[q],
                                                                         in1=mb_t[q], op=ALU.add),
                  r=[("ma", q), ("mb", q)], w=[("mergedT", ct, g)])
            if g == 0 and ct >= 1 and ct + 1 < 8:
                w4load(ct + 1)
            if g == 2 and ct == 5:
                s.add("pool", dma(Wo, w_out.rearrange("(kt p) c -> p kt c", p=128)), w=["Wo"], dma=True)

    if "mergedT" in tap_aps:
        for ct in range(8):
            for g in range(4):
                s.add("sp", dma(tap_aps["mergedT"][:, ct, g * 512:(g + 1) * 512], mergedT[:, ct, g * 512:(g + 1) * 512]),
                      r=[("mergedT", ct, g)], dma=True)
    if stop_after == "p4a":
        return finish()

    s.barrier()
    ar.off = off_after_merged
    gB2 = ar2.alloc([128, D], F32)
    xn2 = [ar2.alloc([128, D], BF16) for _ in range(3)]
    junk2 = ar2.alloc([128, D], BF16)
    ssq2 = ar2.alloc([128, 16], F32)
    rstd2 = ar2.alloc([128, 16], F32)
    xres = [ar.alloc([128, D], F32) for _ in range(3)]
    x1t = [ar.alloc([128, D], F32) for _ in range(3)]
    s.add("sp", dma(gB2, ffn_g.partition_broadcast(128)), w=["gB"], dma=True)
    h2T = hT_own
    fins = []
    def xres_load(tt):
        s.add("sp", dma(xres[tt % 3], x_own[tt * 128:(tt + 1) * 128, :]), w=[("xres", tt % 3)], dma=True)
    xres_load(0)
    xres_load(1)
    for tt in range(16):
        b = tt % 3
        if tt + 2 < 16:
            xres_load(tt + 2)
        for ch in range(2):
            bank = 2 + 2 * b + ch
            s.add("pe", mmgroup(PS[bank][:], [(mergedT[:, ct, tt * 128:(tt + 1) * 128], Wo[:, ct, ch * 512:(ch + 1) * 512])
                                              for ct in range(8)]),
                  r=["Wo"] + [("mergedT", ct, tt // 4) for ct in range(8)], w=[("ps", bank)])
            s.add("dve", lambda e, b=b, ch=ch, bank=bank: e.scalar_tensor_tensor(
                out=x1t[b][:, ch * 512:(ch + 1) * 512], in0=PS[bank][:], scalar=0.5,
                in1=xres[b][:, ch * 512:(ch + 1) * 512], op0=ALU.mult, op1=ALU.add),
                r=[("ps", bank), ("xres", b)], w=[("x1t", b, ch)])
        s.add("sp", dma(out[tt * 128:(tt + 1) * 128, :], x1t[b]), r=[("x1t", b, 0), ("x1t", b, 1)],
              w=[("out", tt)], dma=True)
        fins.append(norm_transpose(x1t[b], [("x1t", b, 0), ("x1t", b, 1)], gB2, ssq2[:, tt:tt + 1],
                                   rstd2[:, tt:tt + 1], xn2[b], ("xn", b), junk2, tt % 2,
                                   h2T[:, :, tt * 128:(tt + 1) * 128], ("h2T", tt), 100 + tt, defer_tr=True))
        if len(fins) > 1:
            fins.pop(0)()
    while fins:
        fins.pop(0)()

    if "h2T" in tap_aps:
        for kt in range(8):
            for g in range(4):
                s.add("sp", dma(tap_aps["h2T"][:, kt, g * 512:(g + 1) * 512], h2T[:, kt, g * 512:(g + 1) * 512]),
                      r=[("h2T", t) for t in range(16)], dma=True)
    if stop_after == "p4b":
        return finish()

    s.barrier()
    ar.reset()
    aT_x = ar.alloc([128, 6, NOWN], BF16)

    def aT(ft):
        if ft < 8:
            return ohgT[:, ft, :]
        if ft < 16:
            return attT[:, ft - 8, :]
        return aT_x[:, ft - 16, :]
    Wg5 = [ar.alloc([128, 8, 128], BF16) for _ in range(2)]
    Wu5 = [ar.alloc([128, 8, 128], BF16) for _ in range(2)]
    Wd5 = [ar.alloc([128, NFT, 512], BF16) for _ in range(2)]
    ar2.reset()
    th5 = [ar2.alloc([128, 512], F32) for _ in range(2)]
    sg5 = [ar2.alloc([128, 512], F32) for _ in range(2)]
    x1r = [ar2.alloc([128, 512], F32) for _ in range(3)]
    o5 = [ar2.alloc([128, 512], F32) for _ in range(3)]
    w_fg_v = w_fg.rearrange("(kt p) c -> p kt c", p=128)
    w_fu_v = w_fu.rearrange("(kt p) c -> p kt c", p=128)
    w_fd_v = w_fd.rearrange("(ft p) c -> p ft c", p=128)
    it = 0
    for ft in range(NFT):
        p = ft % 2
        s.add("pool", dma(Wg5[p], w_fg_v[:, :, ft * 128:(ft + 1) * 128]), w=[("Wg5", p)], dma=True)
        s.add("pool", dma(Wu5[p], w_fu_v[:, :, ft * 128:(ft + 1) * 128]), w=[("Wu5", p)], dma=True)
        for g in range(4):
            q = it % 2
            it += 1
            b0 = 2 * q
            cols = slice(g * 512, (g + 1) * 512)
            s.add("pe", mmgroup(PS[b0][:], [(Wg5[p][:, kt, :], h2T[:, kt, cols]) for kt in range(8)]),
                  r=[("Wg5", p)] + [("h2T", t) for t in range(4 * g, 4 * g + 4)], w=[("ps", b0)])
            s.add("pe", mmgroup(PS[b0 + 1][:], [(Wu5[p][:, kt, :], h2T[:, kt, cols]) for kt in range(8)]),
                  r=[("Wu5", p)] + [("h2T", t) for t in range(4 * g, 4 * g + 4)], w=[("ps", b0 + 1)])
            s.add("act", lambda e, q=q, b0=b0: e.activation(out=th5[q], in_=PS[b0][:], func=AF.Tanh, scale=0.5),
                  r=[("ps", b0)], w=[("th5", q)])
            s.add("dve", lambda e, q=q, b0=b0: e.scalar_tensor_tensor(out=sg5[q], in0=th5[q], scalar=1.0,
                                                                      in1=PS[b0][:], op0=ALU.add, op1=ALU.mult),
                  r=[("th5", q), ("ps", b0)], w=[("sg5", q)])
            s.add("dve", lambda e, q=q, b0=b0, ft=ft, cols=cols: e.tensor_tensor(out=aT(ft)[:, cols], in0=sg5[q],
                                                                                in1=PS[b0 + 1][:], op=ALU.mult),
                  r=[("sg5", q), ("ps", b0 + 1)], w=[("aT", ft, g)])
    for ch in range(2):
        s.add("pool", dma(Wd5[ch], w_fd_v[:, :, ch * 512:(ch + 1) * 512]), w=[("Wd5", ch)], dma=True)
    def x1r_load(it_):
        ch_, tt_ = it_ // 16, it_ % 16
        s.add("sp", dma(x1r[it_ % 3], out[tt_ * 128:(tt_ + 1) * 128, ch_ * 512:(ch_ + 1) * 512]),
              r=[("out", tt_)], w=[("x1r", it_ % 3)], dma=True)
    x1r_load(0)
    x1r_load(1)
    it = 0
    for ch in range(2):
        for tt in range(16):
            q = it % 3
            bank = 4 + it % 2
            if it + 2 < 32:
                x1r_load(it + 2)
            it += 1
            s.add("pe", mmgroup(PS[bank][:], [(aT(ft)[:, tt * 128:(tt + 1) * 128], Wd5[ch][:, ft, :])
                                              for ft in range(NFT)]),
                  r=[("Wd5", ch)] + [("aT", ft, tt // 4) for ft in range(NFT)], w=[("ps", bank)])
            s.add("dve", lambda e, q=q, bank=bank: e.scalar_tensor_tensor(out=o5[q], in0=PS[bank][:], scalar=0.5,
                                                                          in1=x1r[q], op0=ALU.mult, op1=ALU.add),
                  r=[("ps", bank), ("x1r", q)], w=[("o5", q)])
            s.add("sp", dma(out[tt * 128:(tt + 1) * 128, ch * 512:(ch + 1) * 512], o5[q]),
                  r=[("o5", q), ("out", tt)] if ch == 1 else [("o5", q)],
                  w=[("outf", tt, ch)], dma=True)
    return finish()


def t5_bucket_np(dist):
    n = np.maximum(dist, 0)
    nf = np.maximum(n, 1).astype(np.float32)
    large = 16 + (np.log(nf / np.float32(16)) / np.float32(np.log(8.0)) * np.float32(16)).astype(np.int32)
    large = np.minimum(large, 31)
    return np.where(n < 16, n, large)


def host_constants(rel_bias_table, half):
    s_ = np.arange(128)[:, None]
    q_ = np.arange(128)[None, :]
    d_diag = q_ - s_
    d_off = 128 + q_ - s_
    tab = np.asarray(rel_bias_table, np.float32)
    btab = np.empty((H, 128, 2, 128), np.float32)
    bd = tab[t5_bucket_np(d_diag)]
    bo = tab[t5_bucket_np(d_off)]
    for h in range(H):
        btab[h, :, 0, :] = np.where(d_diag >= 0, bd[:, :, h], np.float32(NEG))
        btab[h, :, 1, :] = bo[:, :, h]
    vmask = np.zeros((128, 16, 16), np.float32)
    for qt in range(16):
        cur = 8 + qt // 2
        for n in range(16):
            if n >= cur or (half == 0 and n < 8):
                vmask[:, qt, n] = NEG
    onehot = np.zeros((16, 16 * 128), np.float32)
    for n in range(16):
        onehot[n, n * 128:(n + 1) * 128] = 1.0
    return btab, vmask, (vmask + np.float32(NEG)).astype(np.float32), onehot


def make_in_maps(inputs):
    x = np.asarray(inputs["x"], np.float32)
    shared = {
        "attn_norm_g": np.ascontiguousarray(inputs["attn_norm_g"], np.float32).reshape(1, D),
        "w_in": np.ascontiguousarray(np.asarray(inputs["w_in"], np.float32)[0]),
        "hg_lb_gamma": np.ascontiguousarray(inputs["hg_lb_gamma"], np.float32),
        "hg_out_norm_g": np.ascontiguousarray(inputs["hg_out_norm_g"], np.float32).reshape(1, DH),
        "q_norm_g": np.ascontiguousarray(inputs["q_norm_g"], np.float32).reshape(1, DH),
        "k_norm_g": np.ascontiguousarray(inputs["k_norm_g"], np.float32).reshape(1, DH),
        "rel_bias_table": np.ascontiguousarray(inputs["rel_bias_table"], np.float32),
        "w_branch_hg": np.ascontiguousarray(np.asarray(inputs["w_branch_hg"], np.float32)[0]),
        "w_branch_attn": np.ascontiguousarray(np.asarray(inputs["w_branch_attn"], np.float32)[0]),
        "w_out": np.ascontiguousarray(np.asarray(inputs["w_out"], np.float32)[0]),
        "ffn_norm_g": np.ascontiguousarray(inputs["ffn_norm_g"], np.float32).reshape(1, D),
        "w_ffn_gate": np.ascontiguousarray(np.asarray(inputs["w_ffn_gate"], np.float32)[0]),
        "w_ffn_up": np.ascontiguousarray(np.asarray(inputs["w_ffn_up"], np.float32)[0]),
        "w_ffn_down": np.ascontiguousarray(np.asarray(inputs["w_ffn_down"], np.float32)[0]),
    }
    consts = [host_constants(inputs["rel_bias_table"], half) for half in range(2)]
    in_maps = []
    for c in range(8):
        b, half = c // 2, c % 2
        if half == 0:
            x_pre = np.zeros((NPRE, D), np.float32)
            x_own = np.ascontiguousarray(x[b, :NOWN])
        else:
            x_pre = np.ascontiguousarray(x[b, :NPRE])
            x_own = np.ascontiguousarray(x[b, NPRE:])
        btab, vmask, vmask2, onehot = consts[half]
        m = dict(shared)
        m.update({"x_pre": x_pre, "x_own": x_own, "vmask": vmask, "vmask2": vmask2, "onehot": onehot, "btab": btab})
        in_maps.append(m)
    return in_maps


def kernel(**inputs):
    nc = build_program()
    in_maps = make_in_maps(inputs)
    res = run_bass_kernel_spmd(nc, in_maps, core_ids=list(range(8)))
    outp = np.empty((4, NTOK, D), np.float32)
    for c in range(8):
        b, half = c // 2, c % 2
        outp[b, half * NOWN:(half + 1) * NOWN] = np.asarray(res.results[c]["out"], np.float32)
    return outp
```
